# Optimizing a Trainium2 kernel written in Bass

```python
import math
import jax, jax.numpy as jnp
from jax import lax
import numpy as np

D_MODEL = 1024
BATCH = 4
SEQ = 4096
DEPTH = 2

D_MIX = D_MODEL
HEAD_DIM = 64
RET_HEADS = 4
RET_WIDTH = RET_HEADS * HEAD_DIM
RET_CHUNK = 128
ROPE_BASE = 10000.0
NSA_HEADS = 8
NSA_KV_HEADS = 2
NSA_GROUP = NSA_HEADS // NSA_KV_HEADS
NSA_WIDTH = NSA_HEADS * HEAD_DIM
NSA_KV_WIDTH = NSA_KV_HEADS * HEAD_DIM
CMP_LEN = 32
CMP_STRIDE = 16
SEL_LEN = 64
N_SELECT = 16
WINDOW = 512
NSA_QBLOCK = 64
N_BRANCH = 3
FORCE_SCORE = 1.0e4
LRU_WIDTH = D_MIX - RET_WIDTH - NSA_WIDTH
LRU_BLOCKS = 4
LRU_BLOCK_DIM = LRU_WIDTH // LRU_BLOCKS
CONV_WIDTH = 4
LRU_C = 8.0
DEEPNORM_ALPHA = (2.0 * DEPTH) ** 0.25
DEEPNORM_BETA = (8.0 * DEPTH) ** -0.25
LN_EPS = 1e-5

IN_SIZES = (RET_WIDTH,) * 4 + (NSA_WIDTH,) + (NSA_KV_WIDTH,) * 6 + (NSA_WIDTH, NSA_HEADS * N_BRANCH, LRU_WIDTH, LRU_WIDTH)
D_IN = sum(IN_SIZES)
IN_SPLIT_POINTS = tuple(int(v) for v in np.cumsum(IN_SIZES)[:-1])

kernel_name = "hybrid_retention_nsa_rglru_deepnorm"


def _normalize(x):
    xf = x.astype(jnp.float32)
    mu = jnp.mean(xf, axis=-1, keepdims=True)
    var = jnp.mean(jnp.square(xf - mu), axis=-1, keepdims=True)
    return (xf - mu) * lax.rsqrt(var + LN_EPS)


def layer_norm(x, g, b):
    return (_normalize(x) * g.astype(jnp.float32) + b.astype(jnp.float32)).astype(x.dtype)


def rotary(x, pos):
    half = x.shape[-1] // 2
    inv = 1.0 / (ROPE_BASE ** (jnp.arange(half, dtype=jnp.float32) / half))
    ang = pos.astype(jnp.float32)[:, None] * inv[None, :]
    cos, sin = jnp.cos(ang), jnp.sin(ang)
    x1, x2 = x[..., :half], x[..., half:]
    return jnp.concatenate([x1 * cos - x2 * sin, x1 * sin + x2 * cos], axis=-1).astype(x.dtype)


def masked_softmax(s, mask):
    logits = jnp.where(mask, s.astype(jnp.float32), -1e30)
    return jax.nn.softmax(logits, axis=-1)


def retention(q, k, v):
    B, S, _ = q.shape
    dt = q.dtype
    H, d, C = RET_HEADS, HEAD_DIM, RET_CHUNK
    nc = S // C

    def heads(t):
        return t.reshape(B, S, H, d).transpose(0, 2, 1, 3)

    pos = jnp.arange(S)
    qh = rotary(heads(q), pos)
    kh = rotary(heads(k), pos) * (d ** -0.5)
    vh = heads(v)
    log_g = jnp.log(1.0 - 2.0 ** (-5.0 - jnp.arange(H, dtype=jnp.float32)))
    idx = jnp.arange(C, dtype=jnp.float32)
    diff = idx[:, None] - idx[None, :]
    inner_decay = jnp.where(diff >= 0, jnp.exp(jnp.maximum(diff, 0.0)[None] * log_g[:, None, None]), 0.0).astype(dt)
    key_decay = jnp.exp((C - 1.0 - idx)[None, :] * log_g[:, None]).astype(dt)
    query_decay = jnp.exp((idx + 1.0)[None, :] * log_g[:, None]).astype(dt)
    chunk_decay = jnp.exp(C * log_g).astype(dt)

    qc = qh.reshape(B, H, nc, C, d)
    kc = kh.reshape(B, H, nc, C, d)
    vc = vh.reshape(B, H, nc, C, d)
    scores = jnp.einsum('bhnid,bhnjd->bhnij', qc, kc) * inner_decay[None, :, None]
    inner = jnp.einsum('bhnij,bhnjd->bhnid', scores, vc)
    kv = jnp.einsum('bhnjd,bhnje->nbhde', kc * key_decay[None, :, None, :, None], vc)

    def step(state, kv_c):
        return state * chunk_decay[None, :, None, None] + kv_c, state

    _, states = lax.scan(step, jnp.zeros((B, H, d, d), dt), kv)
    cross = jnp.einsum('bhnid,nbhde->bhnie', qc, states) * query_decay[None, :, None, :, None]
    y = (inner + cross).reshape(B, H, S, d)
    y = _normalize(y).astype(dt)
    return y.transpose(0, 2, 1, 3).reshape(B, S, RET_WIDTH)


def nsa(xq, k_c, v_c, k_s, v_s, k_w, v_w, gate_logits, cmp_pos, cmp_w):
    B, S, _ = xq.shape
    dt = xq.dtype
    G, R, dh, QB = NSA_KV_HEADS, NSA_GROUP, HEAD_DIM, NSA_QBLOCK
    q = xq.reshape(B, S, G, R, dh).transpose(0, 2, 3, 1, 4) * (dh ** -0.5)

    def kvh(t):
        return t.reshape(B, S, G, dh).transpose(0, 2, 1, 3)

    n_cmp = (S - CMP_LEN) // CMP_STRIDE + 1
    tok = np.arange(n_cmp)[:, None] * CMP_STRIDE + np.arange(CMP_LEN)[None, :]

    def compress(t, pos, w):
        blocks = t[:, :, tok] + pos
        return blocks.reshape(B, G, n_cmp, CMP_LEN * dh) @ w

    kc = compress(kvh(k_c), cmp_pos[0], cmp_w[0])
    vc = compress(kvh(v_c), cmp_pos[1], cmp_w[1])
    cmp_end = np.arange(n_cmp) * CMP_STRIDE + CMP_LEN - 1

    n_sel = S // SEL_LEN
    ci = np.arange(n_cmp)[:, None]
    sj = np.arange(n_sel)[None, :]
    overlap = np.minimum(ci * CMP_STRIDE + CMP_LEN, sj * SEL_LEN + SEL_LEN) - np.maximum(ci * CMP_STRIDE, sj * SEL_LEN)
    sel_map = jnp.asarray(np.clip(overlap, 0, None) / CMP_STRIDE, dtype=jnp.float32)
    top = min(N_SELECT, n_sel)
    ks_blocks = kvh(k_s).reshape(B, G, n_sel, SEL_LEN, dh)
    vs_blocks = kvh(v_s).reshape(B, G, n_sel, SEL_LEN, dh)
    gather = jax.vmap(jax.vmap(lambda blk, ix: blk[ix]))

    pad = ((0, 0), (0, 0), (WINDOW, 0), (0, 0))
    kw = jnp.pad(kvh(k_w), pad)
    vw = jnp.pad(kvh(v_w), pad)

    gates = jax.nn.sigmoid(gate_logits.astype(jnp.float32)).astype(dt)
    gates = gates.reshape(B, S, G, R, N_BRANCH).transpose(0, 2, 3, 1, 4)

    def block(i):
        start = i * QB
        t = start + jnp.arange(QB)
        qb = lax.dynamic_slice_in_dim(q, start, QB, axis=3)
        gb = lax.dynamic_slice_in_dim(gates, start, QB, axis=3)
        m_c = cmp_end[None, :] <= t[:, None]
        p_c = masked_softmax(jnp.einsum('bgrqd,bgnd->bgrqn', qb, kc), m_c)
        p_c = p_c * jnp.any(m_c, axis=-1)[:, None]
        o_c = jnp.einsum('bgrqn,bgnd->bgrqd', p_c.astype(dt), vc)
        imp = jnp.einsum('bgrqn,nj->bgqj', p_c, sel_map)
        j = jnp.arange(n_sel)[None, :]
        cur = (t // SEL_LEN)[:, None]
        forced = (j == 0) | (j == cur) | (j == cur - 1)
        causal = j * SEL_LEN <= t[:, None]
        imp = jnp.where(forced, FORCE_SCORE, jnp.where(causal, imp, -FORCE_SCORE))
        _, idx = lax.top_k(imp, top)
        k_sel = gather(ks_blocks, idx).reshape(B, G, QB, top * SEL_LEN, dh)
        v_sel = gather(vs_blocks, idx).reshape(B, G, QB, top * SEL_LEN, dh)
        kpos = (idx[..., None] * SEL_LEN + jnp.arange(SEL_LEN)).reshape(B, G, QB, top * SEL_LEN)
        m_s = (kpos <= t[:, None])[:, :, None]
        p_s = masked_softmax(jnp.einsum('bgrqd,bgqkd->bgrqk', qb, k_sel), m_s)
        o_s = jnp.einsum('bgrqk,bgqkd->bgrqd', p_s.astype(dt), v_sel)
        kwb = lax.dynamic_slice_in_dim(kw, start, WINDOW + QB, axis=2)
        vwb = lax.dynamic_slice_in_dim(vw, start, WINDOW + QB, axis=2)
        jpos = start - WINDOW + jnp.arange(WINDOW + QB)
        m_w = (jpos[None, :] >= 0) & (jpos[None, :] <= t[:, None]) & (jpos[None, :] > t[:, None] - WINDOW)
        p_w = masked_softmax(jnp.einsum('bgrqd,bgkd->bgrqk', qb, kwb), m_w)
        o_w = jnp.einsum('bgrqk,bgkd->bgrqd', p_w.astype(dt), vwb)
        return gb[..., 0:1] * o_c + gb[..., 1:2] * o_s + gb[..., 2:3] * o_w

    out = lax.map(block, jnp.arange(S // QB))
    return out.transpose(1, 0, 4, 2, 3, 5).reshape(B, S, NSA_WIDTH)


def rg_lru(x, conv_w, conv_b, w_a, b_a, w_x, b_x, lam):
    B, S, C = x.shape
    xc = lax.conv_general_dilated(x, conv_w[:, None, :], window_strides=(1,),
                                  padding=((CONV_WIDTH - 1, 0),),
                                  dimension_numbers=('NWC', 'WIO', 'NWC'),
                                  feature_group_count=C) + conv_b
    xb = xc.reshape(B, S, LRU_BLOCKS, LRU_BLOCK_DIM)
    r = jax.nn.sigmoid((jnp.einsum('bsnd,nde->bsne', xb, w_a).reshape(B, S, C) + b_a).astype(jnp.float32))
    i = jax.nn.sigmoid((jnp.einsum('bsnd,nde->bsne', xb, w_x).reshape(B, S, C) + b_x).astype(jnp.float32))
    log_a = -LRU_C * r * jax.nn.softplus(-lam.astype(jnp.float32))
    a = jnp.exp(log_a)
    u = jnp.sqrt(-jnp.expm1(2.0 * log_a)) * (i * xc.astype(jnp.float32))

    def combine(lhs, rhs):
        a1, b1 = lhs
        a2, b2 = rhs
        return a1 * a2, a2 * b1 + b2

    _, h = lax.associative_scan(combine, (a, u), axis=1)
    return h.astype(x.dtype)


def hybrid_layer(x, w_in, w_out, ln_g, ln_b, cmp_pos, cmp_w, conv_w, conv_b, w_a, b_a, w_x, b_x, lam):
    proj = x @ w_in
    (rq, rk, rv, rg, nq, nkc, nvc, nks, nvs, nkw, nvw, ng, ngl, lx, lg) = jnp.split(proj, IN_SPLIT_POINTS, axis=-1)
    y_ret = retention(rq, rk, rv) * jax.nn.silu(rg)
    y_nsa = nsa(nq, nkc, nvc, nks, nvs, nkw, nvw, ngl, cmp_pos, cmp_w) * jax.nn.silu(ng)
    y_lru = rg_lru(lx, conv_w, conv_b, w_a, b_a, w_x, b_x, lam) * jax.nn.silu(lg)
    y = jnp.concatenate([y_ret, y_nsa, y_lru], axis=-1) @ w_out
    return layer_norm(DEEPNORM_ALPHA * x + y, ln_g, ln_b)


def setup_inputs(seed: int = 0) -> dict:
    key = jax.random.key(seed)
    ks = jax.random.split(key, 16)
    f32 = jnp.float32
    x = jax.random.normal(ks[0], (BATCH, SEQ, D_MODEL), f32)
    w_in = jax.random.normal(ks[1], (DEPTH, D_MODEL, D_IN), f32) * D_MODEL ** -0.5
    w_out = jax.random.normal(ks[2], (DEPTH, D_MIX, D_MODEL), f32) * (D_MIX ** -0.5) * DEEPNORM_BETA
    ln_g = 1.0 + 0.02 * jax.random.normal(ks[3], (DEPTH, D_MODEL), f32)
    ln_b = 0.02 * jax.random.normal(ks[4], (DEPTH, D_MODEL), f32)
    nsa_cmp_pos = 0.1 * jax.random.normal(ks[5], (DEPTH, 2, CMP_LEN, HEAD_DIM), f32)
    nsa_cmp_w = jax.random.normal(ks[6], (DEPTH, 2, CMP_LEN * HEAD_DIM, HEAD_DIM), f32) * (CMP_LEN * HEAD_DIM) ** -0.5
    lru_conv_w = jax.random.normal(ks[7], (DEPTH, CONV_WIDTH, LRU_WIDTH), f32) * CONV_WIDTH ** -0.5
    lru_conv_b = 0.02 * jax.random.normal(ks[8], (DEPTH, LRU_WIDTH), f32)
    lru_w_a = jax.random.normal(ks[9], (DEPTH, LRU_BLOCKS, LRU_BLOCK_DIM, LRU_BLOCK_DIM), f32) * LRU_BLOCK_DIM ** -0.5
    lru_b_a = 0.02 * jax.random.normal(ks[10], (DEPTH, LRU_WIDTH), f32)
    lru_w_x = jax.random.normal(ks[11], (DEPTH, LRU_BLOCKS, LRU_BLOCK_DIM, LRU_BLOCK_DIM), f32) * LRU_BLOCK_DIM ** -0.5
    lru_b_x = 0.02 * jax.random.normal(ks[12], (DEPTH, LRU_WIDTH), f32)
    u = jax.random.uniform(ks[13], (DEPTH, LRU_WIDTH), f32, minval=0.9, maxval=0.999)
    a0 = u ** (1.0 / LRU_C)
    lru_lambda = jnp.log(a0) - jnp.log1p(-a0)
    return {"x": x, "w_in": w_in, "w_out": w_out, "ln_g": ln_g, "ln_b": ln_b,
            "nsa_cmp_pos": nsa_cmp_pos, "nsa_cmp_w": nsa_cmp_w,
            "lru_conv_w": lru_conv_w, "lru_conv_b": lru_conv_b,
            "lru_w_a": lru_w_a, "lru_b_a": lru_b_a, "lru_w_x": lru_w_x, "lru_b_x": lru_b_x,
            "lru_lambda": lru_lambda}


def reference(x, w_in, w_out, ln_g, ln_b, nsa_cmp_pos, nsa_cmp_w, lru_conv_w, lru_conv_b,
              lru_w_a, lru_b_a, lru_w_x, lru_b_x, lru_lambda):
    for l in range(DEPTH):
        x = hybrid_layer(x, w_in[l], w_out[l], ln_g[l], ln_b[l], nsa_cmp_pos[l], nsa_cmp_w[l],
                         lru_conv_w[l], lru_conv_b[l], lru_w_a[l], lru_b_a[l], lru_w_x[l], lru_b_x[l],
                         lru_lambda[l])
    return x
```

```python
import math
from contextlib import ExitStack

import numpy as np
import ml_dtypes
import concourse.bass as bass
import concourse.mybir as mybir
from concourse.bass_utils import run_bass_kernel_spmd

F32 = mybir.dt.float32
BF16 = mybir.dt.bfloat16
ALU = mybir.AluOpType
AF = mybir.ActivationFunctionType

S = 4096
DM = 1024
NEG = -30000.0
DBG = {"nblocks": 8, "ret": True, "lru": True, "cmp": True, "nq": 32}

_AP_KW = ("in_", "in0", "in1", "lhsT", "rhs", "scalar", "scalar1", "scalar2", "bias", "scale",
          "data0", "data1", "initial", "in_to_replace", "in_values", "identity")
_ESZ = {F32: 4, BF16: 2}


def _is_ap(v):
    return hasattr(v, "tensor") and hasattr(v, "ap") and hasattr(v, "offset")


def _region(ap):
    t = ap.tensor
    name = t.name
    pat = list(ap.ap)
    esz = _ESZ.get(ap.dtype, 4)
    if "DRam" in type(t).__name__:
        lo = ap.offset
        hi = lo + sum((c - 1) * abs(s) for s, c in pat) + 1
        return (name, 0, 1, lo * esz, hi * esz, False)
    pstep, pcnt = pat[0]
    p0 = ap.start_partition()
    if pstep == 0:
        foff = ap.offset
    else:
        foff = ap.offset - p0 * pstep
    lo = foff
    hi = lo + sum((c - 1) * abs(s) for s, c in pat[1:]) + 1
    return (name, p0, p0 + pcnt, lo * esz, hi * esz, "PSum" in type(t).__name__)


class MK:
    ENG = ("tensor", "vector", "scalar", "gpsimd", "sync")

    def __init__(self, nc):
        self.nc = nc
        self.esem = {e: nc.alloc_semaphore("es_" + e) for e in self.ENG}
        self.nidx = {e: 0 for e in self.ENG}
        self.base = {e: 0 for e in self.ENG}
        self.dsem = {}
        self.dcount = {}
        self.n_inst = 0
        self.sfx = ""
        self.log = []
        self._reset()

    def _reset(self):
        self.streams = {e: [] for e in self.ENG}
        self.needed = {e: set() for e in self.ENG}
        self.known = {e: {} for e in self.ENG}
        self.acc = {}

    def _deps_and_record(self, tok, reads, writes):
        deps = {}
        rr = [_region(ap) for ap in reads]
        wr = [_region(ap) for ap in writes]
        for (name, p0, p1, lo, hi, ps) in rr:
            for r in self.acc.setdefault(name, []):
                if (ps and r[4] != tok[0]) or (r[6] and r[0] < p1 and p0 < r[1] and r[2] < hi and lo < r[3]):
                    if deps.get(r[4], -1) < r[5]:
                        deps[r[4]] = r[5]
        for (name, p0, p1, lo, hi, ps) in wr:
            for r in self.acc.setdefault(name, []):
                if (ps and r[4] != tok[0]) or (r[0] < p1 and p0 < r[1] and r[2] < hi and lo < r[3]):
                    if deps.get(r[4], -1) < r[5]:
                        deps[r[4]] = r[5]
        for (name, p0, p1, lo, hi, ps) in rr + wr:
            if ps:
                recs = self.acc[name]
                recs[:] = [r for r in recs if r[4] == tok[0]]
        for (name, p0, p1, lo, hi, ps) in rr:
            recs = self.acc[name]
            recs[:] = [r for r in recs if not ((not r[6]) and r[4] == tok[0] and p0 <= r[0] and r[1] <= p1
                                               and lo <= r[2] and r[3] <= hi)]
            recs.append((p0, p1, lo, hi, tok[0], tok[1], False))
        for (name, p0, p1, lo, hi, ps) in wr:
            recs = self.acc[name]
            recs[:] = [r for r in recs if not (p0 <= r[0] and r[1] <= p1 and lo <= r[2] and r[3] <= hi)]
            recs.append((p0, p1, lo, hi, tok[0], tok[1], True))
        return deps

    def _emit_waits(self, eng, deps):
        kn = self.known[eng]
        self._lw = []
        for k, v in deps.items():
            if k.startswith("d:"):
                v = self.dcount[k]
            if eng == "tensor" and k == "e:tensor":
                continue
            if kn.get(k, -1) >= v:
                continue
            kn[k] = v
            self._lw.append(k)
            if k.startswith("e:"):
                self.needed[k[2:]].add(v)
            self.streams[eng].append(("wait", k, v))

    def op(self, eng, method, **kw):
        if DBG.get("limit") is not None and self.n_inst >= DBG["limit"]:
            return
        writes = [kw["out"]] if "out" in kw else []
        if kw.get("accum_out", None) is not None:
            writes.append(kw["accum_out"])
        if method == "memset":
            writes.append(kw["ap"])
        reads = [kw[k] for k in _AP_KW if k in kw and _is_ap(kw[k])]
        reads += list(kw.pop("_reads", ()))
        writes += list(kw.pop("_writes", ()))
        idx = self.nidx[eng]
        self.nidx[eng] += 1
        deps = self._deps_and_record(("e:" + eng, idx), reads, writes)
        self._emit_waits(eng, deps)
        self.streams[eng].append(("op", method, kw, idx))
        if DBG.get("log"):
            o = kw.get("out", kw.get("ap"))
            self.log.append((eng, method, o.tensor.name if o is not None else "", tuple(self._lw)))
        self.n_inst += 1

    def dma(self, eng, out, in_, key=None, **kw):
        if DBG.get("limit") is not None and self.n_inst >= DBG["limit"] and not kw.pop("_force", False):
            return
        kw.pop("_force", None)
        if key is None:
            key = out.tensor.name if "DRam" not in type(out.tensor).__name__ else in_.tensor.name
            if self.sfx and key.endswith(self.sfx):
                key = key[:-len(self.sfx)]
        key = "d:" + key
        if key not in self.dsem:
            self.dsem[key] = self.nc.alloc_semaphore("ds_" + key[2:])
            self.dcount[key] = 0
        val = self.dcount[key] + 16
        deps = self._deps_and_record((key, val), [in_], [out])
        self._emit_waits(eng, deps)
        self.dcount[key] = val
        self.streams[eng].append(("dma", out, in_, key, kw))
        self.n_inst += 1

    def coll(self, kind, rg, in_, out, key="cc"):
        eng = "gpsimd"
        key = "d:" + key
        if key not in self.dsem:
            self.dsem[key] = self.nc.alloc_semaphore("ds_" + key[2:])
            self.dcount[key] = 0
        val = self.dcount[key] + 16
        deps = self._deps_and_record((key, val), [in_], [out])
        self._emit_waits(eng, deps)
        self.dcount[key] = val
        self.streams[eng].append(("coll", kind, rg, in_, out, key))
        self.n_inst += 1

    def end_phase(self):
        for k, v in self.dcount.items():
            if self.known["sync"].get(k, -1) < v:
                self.streams["sync"].append(("wait", k, v))
        valmap = {}
        for e in self.ENG:
            s = sorted(self.needed[e])
            valmap[e] = {idx: self.base[e] + i + 1 for i, idx in enumerate(s)}

        def run_stream(e, engobj):
            vm = valmap[e]
            for item in self.streams[e]:
                if item[0] == "wait":
                    _, k, v = item
                    if k.startswith("e:"):
                        engobj.wait_ge(self.esem[k[2:]], valmap[k[2:]][v])
                    else:
                        engobj.wait_ge(self.dsem[k], v)
                elif item[0] == "op":
                    _, method, kw, idx = item
                    inst = getattr(engobj, method)(**kw)
                    if idx in vm:
                        inst.then_inc(self.esem[e], 1)
                elif item[0] == "coll":
                    _, kind, rg, in_, out, key = item
                    engobj.collective_compute(kind, ALU.bypass, replica_groups=rg, ins=[in_], outs=[out]).then_inc(self.dsem[key], 16)
                else:
                    _, out, in_, key, kw = item
                    engobj.dma_start(out=out, in_=in_, **kw).then_inc(self.dsem[key], 16)

        with self.nc.Block() as blk:
            for e in self.ENG:
                if not self.streams[e]:
                    continue
                getattr(blk, e)(lambda engobj, e=e: run_stream(e, engobj))
        for e in self.ENG:
            self.base[e] += len(self.needed[e])
        self._reset()


OFF = dict(rq=0, rk=256, rv=512, rg=768, nq=1024, nkc=1536, nvc=1664, nks=1792, nvs=1920, nkw=2048,
           nvw=2176, ng=2304, ngl=2816, lx=2840, lg=3096)
NFM = 9
GLC = NFM * 128
TM0 = GLC + 12
NCOL = TM0 + 640


def core_columns(hf):
    r = np.arange
    g = hf
    cols = []
    cols += list(OFF["nq"] + g * 256 + r(128))
    cols += list(OFF["nq"] + g * 256 + 128 + r(128))
    cols += list(OFF["nks"] + g * 64 + r(64)) * 2
    cols += list(OFF["nkw"] + g * 64 + r(64)) * 2
    cols += list(OFF["nkc"] + g * 64 + r(64)) + list(OFF["nvc"] + g * 64 + r(64))
    cols += list(OFF["ng"] + g * 256 + r(128))
    cols += list(OFF["ng"] + g * 256 + 128 + r(128))
    cols += list(OFF["lx"] + hf * 128 + r(128))
    cols += list(OFF["lg"] + hf * 128 + r(128))
    cols += list(OFF["ngl"] + g * 12 + r(12))
    cols += list(OFF["nvs"] + g * 64 + r(64)) + list(OFF["nvw"] + g * 64 + r(64))
    for nm in ("rq", "rk", "rv", "rg"):
        cols += list(OFF[nm] + hf * 128 + r(128))
    assert len(cols) == NCOL
    return np.array(cols)


def wout_rows(hf):
    r = np.arange
    return np.concatenate([hf * 128 + r(128), 256 + hf * 256 + r(256), 768 + hf * 128 + r(128)])


_CONST_CACHE = {}


def make_consts(hf):
    if hf in _CONST_CACHE:
        return _CONST_CACHE[hf]
    c = {}
    half = 32
    inv = (1.0 / (10000.0 ** (np.arange(half, dtype=np.float32) / half))).astype(np.float32)
    ang = np.arange(S, dtype=np.float32)[:, None] * inv[None, :]
    c["cos"] = np.cos(ang).astype(np.float32)
    c["sin"] = np.sin(ang).astype(np.float32)
    hs = np.array([2 * hf, 2 * hf + 1], dtype=np.float32)
    log_g = np.log(1.0 - 2.0 ** (-5.0 - hs)).astype(np.float32)
    idx = np.arange(128, dtype=np.float32)
    kd = np.exp((127.0 - idx)[:, None] * log_g[None, :]) / 8.0
    c["kd"] = kd.astype(np.float32)
    qd = np.exp((idx + 1.0)[None, :] * log_g[:, None])
    c["qd"] = np.repeat(qd, 64, axis=0).astype(np.float32)
    diff = idx[None, :] - idx[:, None]
    dt = np.where(diff[None] >= 0, np.exp(np.maximum(diff, 0.0)[None] * log_g[:, None, None]), 0.0) / 8.0
    c["dt"] = np.ascontiguousarray(dt.transpose(1, 0, 2)).reshape(128, 256).astype(np.float32)
    c["cdv"] = np.repeat(np.exp(128.0 * log_g), 64)[:, None].astype(np.float32)
    n = np.arange(128)[:, None]
    cc = np.arange(2176)[None, :]
    c["mbig"] = (cc >= 16 * n + 31).astype(np.float32)
    k = np.arange(128)[:, None]
    q = np.arange(128)[None, :]
    c["negtri"] = np.tile(np.where(k > q, NEG, 0.0), (1, 4)).astype(np.float32)
    c["negtriw"] = np.tile(np.where(k <= q, NEG, 0.0), (1, 4)).astype(np.float32)
    c["ident"] = np.eye(128, dtype=np.float32)
    ci = np.arange(256)[:, None]
    sj = np.arange(64)[None, :]
    ov = np.minimum(ci * 16 + 32, sj * 64 + 64) - np.maximum(ci * 16, sj * 64)
    sm = np.clip(ov, 0, None) / 16.0
    sm[255] = 0.0
    sma = np.concatenate([sm, np.ones((256, 1))], axis=1)
    sma[255] = 0.0
    c["selmap"] = np.ascontiguousarray(sma.reshape(2, 128, 65).transpose(1, 0, 2)).reshape(128, 130).astype(np.float32)
    rel = np.arange(126)[None, :] - 62
    relcur = (np.arange(128)[:, None] >= 64).astype(np.int64)
    cm = (rel < relcur - 1).astype(np.float32)
    am = np.where(rel == relcur, 2.0e4, np.where(rel == relcur - 1, 1.0e4, np.where(rel > relcur, -1.0e4, 0.0)))
    jj = np.arange(64)[:, None]
    kk = np.arange(4096)[None, :]
    c["esel"] = (jj == kk // 64).astype(np.float32)
    c["cmb"] = cm.astype(np.float32)
    c["amb"] = am.astype(np.float32)
    _CONST_CACHE[hf] = c
    return c


def emit_A(nc, mk, banks, sfx, D):
    xT_d = D["xT"]; w_d = D["w"]
    cos_d = D["cos"]; sin_d = D["sin"]
    kd_d = D["kd"]; qd_d = D["qd"]; dt_d = D["dt"]
    mbig_d = D["mbig"]; negtri_d = D["negtri"]; negtriw_d = D["negtriw"]
    ident_d = D["ident"]; selmap_d = D["selmap"]
    cmb_d = D["cmb"]; amb_d = D["amb"]
    cmpw_d = D["cmpw"]; posT_d = D["posT"]
    lruv_d = D["lruv"]
    wa_d = D["wa"]; wx_d = D["wx"]
    cdv_d = D["cdv"]
    esel_d = D["esel"]
    yt_d = D["yt"]
    mk.sfx = sfx
    pes = ExitStack()

    def A(name, shape, dt):
        return pes.enter_context(nc.sbuf_tensor(name + sfx, list(shape), dt))

    qT = A("qT", [128, 32, 2, 128], BF16)
    ksT = A("ksT", [128, S], BF16)
    kwT = A("kwT", [128, S], BF16)
    kvcT = A("kvcT", [128, S], BF16)
    Vs = A("Vs", [128, 32, 128], BF16)
    Vw = A("Vw", [128, 32, 128], BF16)
    ngs = A("ngs", [128, 2, S], BF16)
    gl = A("gl", [32, S], BF16)
    YT = A("YT", [128, 4, S], BF16)
    mbig = A("mbig_sb", [128, 2176], BF16)
    identb = A("identb", [128, 128], BF16)
    identf = A("identf", [128, 128], F32)
    negtri = A("negtri_sb", [128, 512], BF16)
    negtriw = A("negtriw_sb", [128, 512], BF16)
    selmap = A("selmap_sb", [128, 130], BF16)
    cmb = A("cmb_sb", [128, 126], F32)
    amb = A("amb_sb", [128, 126], F32)
    kcT = A("kcT", [128, 256], BF16)
    Vc = A("Vc", [128, 2, 128], BF16)
    tiny = A("tiny", [128, 1], F32)

    def V(method, **kw):
        mk.op("vector", method, **kw)

    def ACT(method="activation", **kw):
        mk.op("scalar", method, **kw)

    def PE(method="matmul", **kw):
        mk.op("tensor", method, **kw)

    def POOL(method, **kw):
        mk.op("gpsimd", method, **kw)

    es = ExitStack()

    def T(name, shape, dt):
        return es.enter_context(nc.sbuf_tensor(name + sfx, list(shape), dt))

    w_sb = T("w_sb", [128, 8, NCOL], BF16)
    xt = T("xt", [128, 2, 8, 512], BF16)
    cos_sb = T("cos_sb", [128, 32, 32], F32)
    sin_sb = T("sin_sb", [128, 32, 32], F32)
    kd_sb = T("kd_sb", [128, 2], F32)
    qd_sb = T("qd_sb", [128, 128], F32)
    dt_sb = T("dt_sb", [128, 256], F32)
    Wk = T("Wk", [128, 32, 64], BF16)
    posT = T("posT_sb", [128, 32], BF16)
    lruv = T("lruv_sb", [128, 9], F32)
    cdv = T("cdv_sb", [128, 1], F32)
    wa_sb = T("wa_sb", [128, 128], BF16)
    wx_sb = T("wx_sb", [128, 128], BF16)
    qk_sb = T("qk_sb", [128, 2, 256], F32)
    rt = [T(f"rt{i}", [128, 128], F32) for i in range(2)]
    rot2 = T("rot", [128, 2, 256], BF16)
    v_sb = T("v_sb", [128, 2, 128], BF16)
    gs_sb = T("gs_sb", [128, 2, 128], F32)
    kdec2 = T("kdec", [128, 2, 128], BF16)
    qrT = T("qrT", [128, 256], BF16)
    qdT = T("qdT", [128, 128], BF16)
    krT = T("krT", [128, 128], BF16)
    scT_sb = T("scT_sb", [128, 256], BF16)
    state_f = T("state_f", [128, 128], F32)
    state_bf = T("state_bf", [128, 2, 128], BF16)
    st6 = T("st6", [128, 2, 6], F32)
    mv = T("mv", [128, 2, 2], F32)
    rstd = T("rstd", [128, 2], F32)
    yn = T("yn", [128, 128], F32)
    yg = T("yg", [128, 128], BF16)
    lxb = T("lxb", [128, 2, 515], F32)
    xc = T("xc", [128, 512], F32)
    xcb = T("xcb", [128, 512], BF16)
    r_sb = T("r_sb", [128, 512], F32)
    i_sb = T("i_sb", [128, 512], F32)
    a_sb = T("a_sb", [128, 512], F32)
    u_sb = T("u_sb", [128, 512], F32)
    h_sb = T("h_sb", [128, 512], F32)
    hlast = T("hlast", [128, 1], F32)
    lgs = T("lgs", [128, 512], BF16)
    sp = T("sp", [128, 4], F32)
    biask = T("biask", [128, 1], F32)
    biasv = T("biasv", [128, 64], BF16)
    one_f = T("one_f", [128, 1], F32)

    for c in range(8):
        mk.dma("gpsimd", out=w_sb[:, c, :], in_=w_d[c * 128:(c + 1) * 128, :])
    mk.dma("sync", out=cos_sb[:], in_=cos_d.rearrange("(c p) f -> p c f", p=128))
    mk.dma("sync", out=sin_sb[:], in_=sin_d.rearrange("(c p) f -> p c f", p=128))
    mk.dma("sync", out=kd_sb[:], in_=kd_d)
    mk.dma("sync", out=qd_sb[:], in_=qd_d)
    mk.dma("sync", out=dt_sb[:], in_=dt_d)
    mk.dma("sync", out=identf[:], in_=ident_d)
    mk.dma("sync", out=cmb[:], in_=cmb_d)
    mk.dma("sync", out=amb[:], in_=amb_d)
    mk.dma("sync", out=lruv[:], in_=lruv_d)
    mk.dma("sync", out=cdv[:], in_=cdv_d)
    mk.dma("gpsimd", out=mbig[:, 0:1088], in_=mbig_d[:, 0:1088])
    mk.dma("gpsimd", out=mbig[:, 1088:2176], in_=mbig_d[:, 1088:2176])
    mk.dma("gpsimd", out=identb[:], in_=ident_d)
    mk.dma("gpsimd", out=negtri[:], in_=negtri_d)
    mk.dma("gpsimd", out=negtriw[:], in_=negtriw_d)
    mk.dma("gpsimd", out=selmap[:], in_=selmap_d)
    mk.dma("gpsimd", out=wa_sb[:], in_=wa_d)
    mk.dma("gpsimd", out=wx_sb[:], in_=wx_d)
    mk.dma("gpsimd", out=Wk[0:64, :, :], in_=cmpw_d[0].rearrange("(l d) e -> d l e", d=64))
    mk.dma("gpsimd", out=Wk[64:128, :, :], in_=cmpw_d[1].rearrange("(l d) e -> d l e", d=64))
    mk.dma("gpsimd", out=posT[0:64, :], in_=posT_d[0])
    mk.dma("gpsimd", out=posT[64:128, :], in_=posT_d[1])
    V("memset", ap=tiny[0:64, :], constant=0.0)
    V("memset", ap=tiny[64:128, :], constant=1e-18)
    V("memset", ap=qrT[:], constant=0.0)
    V("memset", ap=state_f[:], constant=0.0)
    V("memset", ap=one_f[:], constant=1.0)
    V("memset", ap=Vs[:, :, 64:128], constant=1.0)
    V("memset", ap=Vw[:, :, 64:128], constant=1.0)
    V("memset", ap=Vc[:], constant=0.0)
    V("memset", ap=kcT[:], constant=0.0)
    V("memset", ap=kwT[64:128, :], constant=0.0)
    mk.dma("gpsimd", out=ksT[64:128, 0:2048], in_=esel_d[:, 0:2048], key="esel")
    mk.dma("gpsimd", out=ksT[64:128, 2048:4096], in_=esel_d[:, 2048:4096], key="esel")
    V("memset", ap=lxb[:, 1, 512:515], constant=0.0)
    ACT(out=sp[:, 0:1], in_=lruv[:, 7:8], func=AF.Exp, scale=-1.0)
    V("tensor_scalar", out=sp[:, 0:1], in0=sp[:, 0:1], scalar1=1.0, scalar2=None, op0=ALU.add)
    ACT(out=sp[:, 1:2], in_=sp[:, 0:1], func=AF.Ln)
    V("tensor_scalar", out=sp[:, 2:3], in0=sp[:, 1:2], scalar1=-8.0, scalar2=None, op0=ALU.mult)
    V("tensor_scalar", out=sp[:, 3:4], in0=sp[:, 1:2], scalar1=-16.0, scalar2=None, op0=ALU.mult)

    xT_v = xT_d.rearrange("(c p) t -> p c t", p=128)
    fm_bank = [banks[0], banks[1]]
    fmi = 0
    psA = banks[2]
    psB = banks[3]
    bankT = banks[4]
    tq_ps = bankT[:, 0:64].bitcast(BF16)
    tk_ps = bankT[:, 64:128].bitcast(BF16)
    ty_ps = banks[7][:, 0:64].bitcast(BF16)
    scT_ps = banks[5][:, 0:256]
    y_ps = banks[5][:, 256:384]
    kv_ps = banks[5][:, 384:512]
    ps_r = banks[6]
    ps_i = banks[7]

    NB = DBG["nblocks"]

    def xload(tbn):
        xbn = xt[:, tbn % 2]
        mk.dma("gpsimd", out=xbn[:, 0:4, :], in_=xT_v[:, 0:4, tbn * 512:tbn * 512 + 512], key="xt%d" % (tbn % 2))
        mk.dma("gpsimd", out=xbn[:, 4:8, :], in_=xT_v[:, 4:8, tbn * 512:tbn * 512 + 512], key="xt%d" % (tbn % 2))

    fmc = [0]

    def G(tb, g):
        xb = xt[:, tb % 2]
        t0 = tb * 512
        M = 128 if g < NFM else 12
        ps = fm_bank[fmc[0] % 2]
        fmc[0] += 1
        for c in range(8):
            PE(out=ps[0:M, :], lhsT=w_sb[:, c, g * 128:g * 128 + M], rhs=xb[:, c, :], start=(c == 0), stop=(c == 7))
        if g in (0, 1):
            V("tensor_copy", out=qT[:, tb * 4:(tb + 1) * 4, g, :], in_=ps[:].rearrange("p (a q) -> p a q", a=4))
        elif g == 2:
            ACT(out=ksT[0:64, t0:t0 + 512], in_=ps[0:64, :], func=AF.Copy)
        elif g == 3:
            V("tensor_copy", out=kwT[0:64, t0:t0 + 512], in_=ps[0:64, :])
        elif g == 4:
            ACT(out=kvcT[:, t0:t0 + 512], in_=ps[:], func=AF.Copy)
        elif g in (5, 6):
            ACT(out=ngs[:, g - 5, t0:t0 + 512], in_=ps[:], func=AF.Silu)
        elif g == 7:
            V("tensor_copy", out=lxb[:, tb % 2, 3:515], in_=ps[:])
        elif g == 8:
            ACT(out=lgs[:], in_=ps[:], func=AF.Silu)
        else:
            ACT(out=gl[0:12, t0:t0 + 512], in_=ps[0:12, :], func=AF.Sigmoid)

    def TM(ch):
        tb, tt = ch // 4, ch % 4
        xb = xt[:, tb % 2]
        for c in range(8):
            PE(out=psB[:, 0:128], lhsT=xb[:, c, tt * 128:(tt + 1) * 128], rhs=w_sb[:, c, TM0:TM0 + 128],
               start=(c == 0), stop=(c == 7))
        for c in range(8):
            PE(out=psA[:], lhsT=xb[:, c, tt * 128:(tt + 1) * 128], rhs=w_sb[:, c, TM0 + 128:TM0 + 640],
               start=(c == 0), stop=(c == 7))
        V("tensor_copy", out=Vs[:, ch, 0:64], in_=psB[:, 0:64])
        V("tensor_copy", out=Vw[:, ch, 0:64], in_=psB[:, 64:128])
        ACT(out=qk_sb[:, ch % 2, :], in_=psA[:, 0:256], func=AF.Copy)
        ACT(out=v_sb[:, ch % 2, :], in_=psA[:, 256:384], func=AF.Copy)
        ACT(out=gs_sb[:, ch % 2, :], in_=psA[:, 384:512], func=AF.Silu)

    def R0(ch):
        qs4 = qk_sb[:, ch % 2, :].rearrange("p (a b c) -> p a b c", a=4, b=2)
        rot = rot2[:, ch % 2, :]
        kdec = kdec2[:, ch % 2, :]
        rot4 = rot.rearrange("p (a b c) -> p a b c", a=4, b=2)
        cosb = cos_sb[:, ch, :].unsqueeze(1).to_broadcast([128, 4, 32])
        sinb = sin_sb[:, ch, :].unsqueeze(1).to_broadcast([128, 4, 32])
        r0 = rt[0][:].rearrange("p (a c) -> p a c", a=4)
        r1 = rt[1][:].rearrange("p (a c) -> p a c", a=4)
        V("tensor_tensor", out=r0, in0=qs4[:, :, 0, :], in1=cosb, op=ALU.mult)
        V("tensor_tensor", out=r1, in0=qs4[:, :, 1, :], in1=sinb, op=ALU.mult)
        V("tensor_tensor", out=rot4[:, :, 0, :], in0=r0, in1=r1, op=ALU.subtract)
        V("tensor_tensor", out=r0, in0=qs4[:, :, 0, :], in1=sinb, op=ALU.mult)
        V("tensor_tensor", out=r1, in0=qs4[:, :, 1, :], in1=cosb, op=ALU.mult)
        V("tensor_tensor", out=rot4[:, :, 1, :], in0=r0, in1=r1, op=ALU.add)
        POOL("tensor_tensor", out=kdec.rearrange("p (h d) -> p h d", h=2),
             in0=rot[:, 128:256].rearrange("p (h d) -> p h d", h=2),
             in1=kd_sb[:].unsqueeze(2).to_broadcast([128, 2, 64]), op=ALU.mult)

    def R1(ch):
        PE("transpose", out=tq_ps, in_=rot2[:, ch % 2, 0:128], identity=identb[:])
        PE("transpose", out=tk_ps, in_=rot2[:, ch % 2, 128:256], identity=identb[:])
        ACT(out=qrT[0:64, 0:128], in_=tq_ps[0:64, :], func=AF.Copy)
        ACT(out=qrT[64:128, 128:256], in_=tq_ps[64:128, :], func=AF.Copy)
        V("tensor_tensor", out=qdT[:], in0=tq_ps, in1=qd_sb[:], op=ALU.mult)
        ACT(out=krT[:], in_=tk_ps, func=AF.Copy)

    def R2(ch):
        PE(out=scT_ps, lhsT=krT[:], rhs=qrT[:], start=True, stop=True)
        V("tensor_tensor", out=scT_sb[:], in0=scT_ps, in1=dt_sb[:], op=ALU.mult)

    def R3(ch):
        vv = v_sb[:, ch % 2, :]
        for h in range(2):
            PE(out=y_ps[:, h * 64:(h + 1) * 64], lhsT=scT_sb[:, h * 128:(h + 1) * 128], rhs=vv[:, h * 64:(h + 1) * 64],
               start=(h == 0), stop=(ch == 0 and h == 1), skip_group_check=True)
        if ch > 0:
            PE(out=y_ps, lhsT=qdT[:], rhs=state_bf[:, ch % 2, :], start=False, stop=True, skip_group_check=True)
        PE(out=kv_ps, lhsT=kdec2[:, ch % 2, :], rhs=vv, start=True, stop=True)
        for h in range(2):
            V("bn_stats", out=st6[:, h, :], in_=y_ps[:, h * 64:(h + 1) * 64])
            V("bn_aggr", out=mv[:, h, :], in_=st6[:, h, :])
        V("tensor_scalar", out=rstd[:], in0=mv[:, :, 1], scalar1=1e-5, scalar2=None, op0=ALU.add)
        ACT(out=rstd[:], in_=rstd[:], func=AF.Sqrt)
        V("reciprocal", out=rstd[:], in_=rstd[:])
        for h in range(2):
            V("tensor_scalar", out=yn[:, h * 64:(h + 1) * 64], in0=y_ps[:, h * 64:(h + 1) * 64],
              scalar1=mv[:, h, 0:1], scalar2=rstd[:, h:h + 1], op0=ALU.subtract, op1=ALU.mult)
        for h in range(2):
            hp = slice(64 * h, 64 * h + 64)
            hc = slice(64 * h, 64 * h + 64)
            if ch == 0:
                V("tensor_copy", out=state_f[hp, hc], in_=kv_ps[hp, 64 * h:64 * h + 64])
            else:
                V("scalar_tensor_tensor", out=state_f[hp, hc], in0=state_f[hp, hc], scalar=cdv[hp, 0:1],
                  in1=kv_ps[hp, 64 * h:64 * h + 64], op0=ALU.mult, op1=ALU.add)
        POOL("tensor_copy", out=state_bf[:, (ch + 1) % 2, :], in_=state_f[:])
        V("tensor_tensor", out=yg[:], in0=yn[:], in1=gs_sb[:, ch % 2, :], op=ALU.mult)

    def R4(ch):
        PE("transpose", out=ty_ps, in_=yg[:], identity=identb[:])
        ACT(out=YT[:, 0, ch * 128:(ch + 1) * 128], in_=ty_ps, func=AF.Copy)

    def LRU_a(tb):
        lx = lxb[:, tb % 2]
        if tb == 0:
            V("memset", ap=lx[:, 0:3], constant=0.0)
        else:
            V("tensor_copy", out=lx[:, 0:3], in_=lxb[:, (tb - 1) % 2, 512:515])
        V("tensor_scalar", out=xc[:], in0=lx[:, 0:512], scalar1=lruv[:, 0:1], scalar2=lruv[:, 4:5], op0=ALU.mult, op1=ALU.add)
        for w in range(1, 4):
            V("scalar_tensor_tensor", out=xc[:], in0=lx[:, w:w + 512], scalar=lruv[:, w:w + 1], in1=xc[:],
              op0=ALU.mult, op1=ALU.add)
        V("tensor_copy", out=xcb[:], in_=xc[:])

    def LRU_b(tb):
        t0 = tb * 512
        PE(out=ps_r[:], lhsT=wa_sb[:], rhs=xcb[:], start=True, stop=True)
        PE(out=ps_i[:], lhsT=wx_sb[:], rhs=xcb[:], start=True, stop=True)
        ACT(out=r_sb[:], in_=ps_r[:], func=AF.Sigmoid, bias=lruv[:, 5:6])
        ACT(out=i_sb[:], in_=ps_i[:], func=AF.Sigmoid, bias=lruv[:, 6:7])
        ACT(out=a_sb[:], in_=r_sb[:], func=AF.Exp, scale=sp[:, 2:3])
        ACT(out=r_sb[:], in_=r_sb[:], func=AF.Exp, scale=sp[:, 3:4])
        ACT(out=r_sb[:], in_=r_sb[:], func=AF.Sqrt, scale=-1.0, bias=one_f[:])
        POOL("tensor_tensor", out=u_sb[:], in0=i_sb[:], in1=xc[:], op=ALU.mult)
        V("tensor_tensor", out=u_sb[:], in0=u_sb[:], in1=r_sb[:], op=ALU.mult)
        init = 0.0 if tb == 0 else hlast[:, 0:1]
        V("tensor_tensor_scan", out=h_sb[:], data0=a_sb[:], data1=u_sb[:], initial=init, op0=ALU.mult, op1=ALU.add)
        V("tensor_copy", out=hlast[:], in_=h_sb[:, 511:512])
        V("tensor_tensor", out=YT[:, 3, t0:t0 + 512], in0=h_sb[:], in1=lgs[:], op=ALU.mult)

    FILL = [[0, 1, 2], [3, 4, 5], [6, 7], [8, 9]]
    NCH = NB * 4
    for ch in range(NCH):
        tb, tt = ch // 4, ch % 4
        if tt == 0:
            for tbn in ([0, 1] if tb == 0 else [tb + 1]):
                if tbn < NB:
                    xload(tbn)
        fl = list(FILL[tt])
        TM(ch)
        if DBG["ret"]:
            R0(ch)
        if ch > 1 and DBG["ret"]:
            R4(ch - 2)
        if ch > 0 and DBG["ret"]:
            R1(ch - 1)
        G(tb, fl.pop(0))
        if ch > 0 and DBG["ret"]:
            R2(ch - 1)
        G(tb, fl.pop(0))
        if ch > 0 and DBG["ret"]:
            R3(ch - 1)
        if fl:
            G(tb, fl.pop(0))
        if tt == 3 and DBG["lru"]:
            LRU_a(tb)
        if tt == 0 and tb > 0 and DBG["lru"]:
            LRU_b(tb - 1)
    if DBG["lru"] and NB > 0:
        LRU_b(NB - 1)
    if DBG["ret"] and NCH > 0:
        if NCH > 1:
            R4(NCH - 2)
        R1(NCH - 1); R2(NCH - 1); R3(NCH - 1); R4(NCH - 1)

    cps = banks[0]
    for l in range(32 if DBG["cmp"] else 0):
        PE(out=cps[0:64, 0:255], lhsT=Wk[0:64, l, :],
           rhs=kvcT[0:64, l:l + 16 * 254 + 1:16], start=(l == 0), stop=(l == 31))
    bps = banks[1]
    for l in range(32 if DBG["cmp"] else 0):
        PE(out=bps[0:64, 0:1], lhsT=Wk[0:64, l, :],
           rhs=posT[0:64, l:l + 1], start=(l == 0), stop=(l == 31))
    if DBG["cmp"]:
        V("tensor_copy", out=biask[0:64, :], in_=bps[0:64, 0:1])
        ACT(out=kcT[0:64, 0:255], in_=cps[0:64, 0:255], func=AF.Identity, bias=biask[0:64, :])
    bvp = banks[2]
    for l in range(32 if DBG["cmp"] else 0):
        PE(out=bvp[0:128, 0:64], lhsT=posT[64:128, l:l + 1].to_broadcast([64, 128]),
           rhs=Wk[64:128, l, :], start=(l == 0), stop=(l == 31))
    if DBG["cmp"]:
        V("tensor_copy", out=biasv[:], in_=bvp[:, 0:64])
    for nt in range(2 if DBG["cmp"] else 0):
        nn = 128 if nt == 0 else 127
        vps = banks[3 + nt]
        for l in range(32):
            s0 = nt * 2048 + l
            PE(out=vps[0:nn, 0:64], lhsT=kvcT[64:128, s0:s0 + 16 * (nn - 1) + 1:16], rhs=Wk[64:128, l, :],
               start=(l == 0), stop=(l == 31))
        V("tensor_tensor", out=Vc[0:nn, nt, 0:64], in0=vps[0:nn, 0:64], in1=biasv[0:nn, :], op=ALU.add)
        V("memset", ap=Vc[0:nn, nt, 64:128], constant=1.0)

    mk.end_phase()
    es.close()

    es = ExitStack()
    PT = [T(f"PT{i}", [128, 512], BF16) for i in range(4)]
    PTm = T("PTm", [128, 512], BF16)
    Osb = [[T(f"Osb{i}_{j}", [128, 512], F32) for j in range(2)] for i in range(3)]
    PTmm = [T(f"PTm{j}", [128, 512], BF16) for j in range(2)]
    c_sbs = [T(f"c_sb{j}", [64, 512], F32) for j in range(3)]
    y_sb = T("y_sb", [128, 256], F32)
    tmp_sbs = [T(f"tmp_sb{j}", [128, 256], F32) for j in range(3)]
    A_sb = T("A_sb", [128, 260], F32)
    rdc = T("rdc", [128, 4], F32)
    imp = T("imp", [128, 64], F32)
    imp2 = T("imp2", [128, 64], F32)
    m8 = T("m8", [128, 16], F32)
    nsel = T("nsel", [128, 64], F32)
    qn = T("qn", [128, 2, 512], BF16)
    POOL("memset", ap=qn[:], constant=0.0)

    ST = [banks[0], banks[1]]
    OsB = [banks[2], banks[2]]
    OcwB = banks[3]
    gbBs = [banks[4], banks[5], banks[6]]
    A_ps = banks[7][:, 0:260]
    nT_ps = banks[7][0:64, 384:512]
    sti = [0]
    pti = [0]

    def make_qbd(i):
        qb = qn[0:64, i % 2, :].rearrange("p (j r q) -> p j r q", j=2, r=2)
        POOL("tensor_copy", out=qb[:, :, 0, :], in_=qT[0:64, i, :, :])
        POOL("tensor_copy", out=qb[:, :, 1, :], in_=qT[64:128, i, :, :])

    def score_tile(kT_sb, k0, i, mask=None, dst=None):
        st = ST[sti[0] % 2]
        sti[0] += 1
        PE(out=st[:], lhsT=kT_sb[:, k0:k0 + 128], rhs=qn[:, i % 2, :], start=True, stop=(mask is None), skip_group_check=True)
        if mask is not None:
            lhsT, rhs = mask
            PE(out=st[:], lhsT=lhsT, rhs=rhs, start=False, stop=True, skip_group_check=True)
        if dst is None:
            pt = PT[pti[0] % 4]
            pti[0] += 1
        else:
            pt = dst
        ACT(out=pt[:], in_=st[:], func=AF.Exp, scale=0.125)
        return pt

    def bc4(ap2d, p):
        return ap2d.unsqueeze(1).to_broadcast([p, 4, 128])

    cmp_ptm = {}

    def cmp_A(i):
        make_qbd(i)
        nts = [0] if i < 16 else [0, 1]
        lst = []
        for nt in nts:
            if nt == 0 and i >= 17:
                ptm = score_tile(kcT, nt * 128, i, dst=PTmm[nt])
            else:
                pt = score_tile(kcT, nt * 128, i)
                c0 = 128 * i if nt == 0 else 128 * (i - 16)
                ptm = PTmm[nt]
                V("tensor_tensor", out=ptm[:].rearrange("p (h q) -> p h q", h=4), in0=pt[:].rearrange("p (h q) -> p h q", h=4),
                  in1=bc4(mbig[:, c0:c0 + 128], 128), op=ALU.mult)
            lst.append(ptm)
        cmp_ptm[i] = lst

    def cmp_B(i):
        nts = [0] if i < 16 else [0, 1]
        for nt in nts:
            ptm = cmp_ptm[i][nt]
            PE(out=OcwB[:], lhsT=Vc[:, nt, :], rhs=ptm[:], start=(nt == 0), stop=(nt == nts[-1]))
            for h in range(4):
                PE(out=A_ps[:, h * 65:(h + 1) * 65], lhsT=ptm[:, h * 128:(h + 1) * 128], rhs=selmap[:, nt * 65:(nt + 1) * 65],
                   start=(nt == 0 and h == 0), stop=(nt == nts[-1]), skip_group_check=True)
        ACT(out=Osb[0][i % 2][:], in_=OcwB[:], func=AF.Identity, bias=tiny[:])
        V("tensor_copy", out=A_sb[:], in_=A_ps)
        A3 = A_sb[:].rearrange("p (h c) -> p h c", h=4)
        V("tensor_scalar", out=rdc[:], in0=A3[:, :, 64], scalar1=1e-30, scalar2=None, op0=ALU.max)
        V("reciprocal", out=rdc[:], in_=rdc[:])
        V("tensor_scalar", out=imp[:], in0=A3[:, 0, 0:64], scalar1=rdc[:, 0:1], scalar2=None, op0=ALU.mult)
        for h in range(1, 4):
            V("scalar_tensor_tensor", out=imp[:], in0=A3[:, h, 0:64], scalar=rdc[:, h:h + 1], in1=imp[:],
              op0=ALU.mult, op1=ALU.add)
        cc = 62 - 2 * i
        V("tensor_tensor", out=imp[:], in0=imp[:], in1=cmb[:, cc:cc + 64], op=ALU.mult)
        V("tensor_tensor", out=imp[:], in0=imp[:], in1=amb[:, cc:cc + 64], op=ALU.add)
        V("memset", ap=imp[:, 0:1], constant=3.0e4)
        V("max", out=m8[:, 0:8], in_=imp[:])
        V("match_replace", out=imp2[:], in_to_replace=m8[:, 0:8], in_values=imp[:], imm_value=-3.0e4)
        V("max", out=m8[:, 8:16], in_=imp2[:])
        V("tensor_scalar", out=nsel[:], in0=imp[:], scalar1=m8[:, 15:16], scalar2=NEG, op0=ALU.is_lt, op1=ALU.mult)

    def cmp_C(i):
        PE("transpose", out=nT_ps, in_=nsel[:], identity=identf[:])
        ACT(out=qn[64:128, i % 2, :].rearrange("p (h q) -> p h q", h=4), in_=nT_ps.unsqueeze(1).to_broadcast([64, 4, 128]),
            func=AF.Copy)

    def attn_loop(kT_sb, Vaug, kts, i, Obank, maskf):
        pts = []
        n = len(kts)
        pend = None
        for idx, kt in enumerate(kts):
            pt = score_tile(kT_sb, kt * 128, i, maskf(kt))
            if pend is not None:
                pidx, pkt, ppt = pend
                PE(out=Obank[:], lhsT=Vaug[:, pkt, :], rhs=ppt[:], start=(pidx == 0), stop=False)
            pend = (idx, kt, pt)
        pidx, pkt, ppt = pend
        PE(out=Obank[:], lhsT=Vaug[:, pkt, :], rhs=ppt[:], start=(pidx == 0), stop=True)

    def sel_mask(i):
        def f(kt):
            if kt == i:
                return (identb[:], negtri[:])
            return None
        return f

    def win_mask(i):
        def f(kt):
            if kt == i:
                return (identb[:], negtri[:])
            if kt == i - 4:
                return (identb[:], negtriw[:])
            return None
        return f

    def combine(i, osb_list):
        pg = 0
        c0 = i * 128
        for n_, (br, osb) in enumerate(osb_list):
            gbB = gbBs[br]
            for cb in range(4):
                h = cb
                hb = h * 3 + br
                lhsT = identb[pg:pg + 12, pg + hb:pg + hb + 1].to_broadcast([12, 64])
                PE(out=gbB[0:64, cb * 128:(cb + 1) * 128], lhsT=lhsT, rhs=gl[pg:pg + 12, c0:c0 + 128], start=True, stop=True)
        for n_, (br, osb) in enumerate(osb_list):
            gbB = gbBs[br]
            c_sb = c_sbs[br]
            tmp_sb = tmp_sbs[br]
            if i < 16:
                ACT(out=c_sb[:], in_=osb[64:128, :], func=AF.Ln)
                ACT(out=c_sb[:], in_=c_sb[:], func=AF.Exp, scale=-1.0)
            else:
                V("reciprocal", out=c_sb[:], in_=osb[64:128, :])
            V("tensor_tensor", out=c_sb[:], in0=gbB[0:64, :], in1=c_sb[:], op=ALU.mult)
            dst = y_sb if n_ == 0 else tmp_sb
            o4 = osb[0:64, :].rearrange("p (j r q) -> p j r q", j=2, r=2)
            c4 = c_sb[:].rearrange("p (j r q) -> p j r q", j=2, r=2)
            for hh in range(2):
                POOL("tensor_tensor", out=dst[64 * hh:64 * hh + 64, :].rearrange("p (j q) -> p j q", j=2), in0=o4[:, :, hh, :],
                     in1=c4[:, :, hh, :], op=ALU.mult)
            if n_ > 0:
                POOL("tensor_tensor", out=y_sb[:], in0=y_sb[:], in1=tmp_sb[:], op=ALU.add)
        V("tensor_tensor", out=YT[:, 1:3, i * 128:(i + 1) * 128], in0=y_sb[:].rearrange("p (j q) -> p j q", j=2),
          in1=ngs[:, :, i * 128:(i + 1) * 128], op=ALU.mult)

    NQ = DBG["nq"]
    if NQ > 0:
        cmp_A(0)
        cmp_B(0)
    for i in range(NQ):
        if i + 1 < NQ:
            cmp_A(i + 1)
        cmp_C(i)
        if i > 0:
            combine(i - 1, [(b_, Osb[b_][(i - 1) % 2]) for b_ in DBG.get("branches", (0, 1, 2))])
        kts = [kt for kt in range(i - 4, i + 1) if kt >= 0]
        attn_loop(kwT, Vw, kts, i, OcwB, win_mask(i))
        ACT(out=Osb[2][i % 2][:], in_=OcwB[:], func=AF.Identity, bias=tiny[:])
        if i + 1 < NQ:
            cmp_B(i + 1)
        attn_loop(ksT, Vs, list(range(i + 1)), i, OsB[i % 2], sel_mask(i))
        ACT(out=Osb[1][i % 2][:], in_=OsB[i % 2][:], func=AF.Identity, bias=tiny[:])
    if NQ > 0:
        combine(NQ - 1, [(b_, Osb[b_][(NQ - 1) % 2]) for b_ in DBG.get("branches", (0, 1, 2))])

    for c in range(4):
        mk.dma("sync", out=yt_d[c * 128:(c + 1) * 128, :], in_=YT[:, c, :], _force=True)
    mk.end_phase()
    es.close()
    pes.close()


def emit_B(nc, mk, banks, sfx, D, make_xT):
    yt_d = D["ytf"]; x_d = D["xr"]; wo_d = D["wo"]; g_d = D["lng"]; b_d = D["lnb"]; o_d = D["xo"]
    mk.sfx = sfx
    es = ExitStack()

    def T(name, shape, dt):
        return es.enter_context(nc.sbuf_tensor(name + sfx, list(shape), dt))

    wo = T("wo_sb", [128, 8, DM], BF16)
    gam = T("gam", [128, DM], F32)
    bet = T("bet", [128, DM], F32)
    ytb = T("ytb", [128, 2, 8, 512], BF16)
    xs = [T(f"xs{i}", [128, DM], F32) for i in range(3)]
    zs = [T(f"zs{i}", [128, DM], F32) for i in range(3)]
    ys = [T(f"ys{i}", [128, DM], F32) for i in range(2)]
    identb = T("identb_b", [128, 128], BF16)
    zb = T("zb", [128, DM], BF16)
    xTt = T("xTt", [128, 2, 8, 128], BF16)
    alpha = float((2.0 * 2) ** 0.25)

    mk.dma("gpsimd", out=wo[:, 0:4, :], in_=wo_d.rearrange("(c p) n -> p c n", p=128)[:, 0:4, :])
    mk.dma("gpsimd", out=wo[:, 4:8, :], in_=wo_d.rearrange("(c p) n -> p c n", p=128)[:, 4:8, :])
    mk.dma("gpsimd", out=identb[:], in_=D["ident"])
    mk.dma("sync", out=gam[:], in_=g_d.to_broadcast([128, DM]))
    mk.dma("sync", out=bet[:], in_=b_d.to_broadcast([128, DM]))
    yt_v = yt_d.rearrange("(c p) t -> p c t", p=128)
    xT_v = D["xTo"].rearrange("(c p) t -> p c t", p=128) if make_xT else None
    st6s = [T(f"bst6_{i}", [128, 2, 6], F32) for i in range(2)]
    mvs = [T(f"bmv_{i}", [128, 2], F32) for i in range(2)]
    rstds = [T(f"brstd_{i}", [128, 1], F32) for i in range(2)]
    NTB = S // 128

    def stage1(t):
        tb = t // 4
        if t % 4 == 0:
            for tbn in ([0, 1] if tb == 0 else [tb + 1]):
                if tbn < S // 512:
                    mk.dma("scalar", out=ytb[:, tbn % 2, 0:4, :], in_=yt_v[:, 0:4, tbn * 512:(tbn + 1) * 512], key="ytb%d" % (tbn % 2))
                    mk.dma("scalar", out=ytb[:, tbn % 2, 4:8, :], in_=yt_v[:, 4:8, tbn * 512:(tbn + 1) * 512], key="ytb%d" % (tbn % 2))
        x_sb = xs[t % 3]
        z = zs[t % 3]
        y_sb = ys[t % 2]
        mk.dma("scalar", out=x_sb[:], in_=x_d[t * 128:(t + 1) * 128, :])
        nbk = 6 if make_xT else 8
        for nh in range(2):
            ps = banks[(2 * t + nh) % nbk]
            for c in range(8):
                mk.op("tensor", "matmul", out=ps[:], lhsT=ytb[:, tb % 2, c, (t % 4) * 128:(t % 4 + 1) * 128],
                      rhs=wo[:, c, nh * 512:(nh + 1) * 512], start=(c == 0), stop=(c == 7))
            mk.op("scalar", "activation", out=y_sb[:, nh * 512:(nh + 1) * 512], in_=ps[:], func=AF.Copy)
            mk.op("vector", "scalar_tensor_tensor", out=z[:, nh * 512:(nh + 1) * 512], in0=x_sb[:, nh * 512:(nh + 1) * 512],
                  scalar=alpha, in1=y_sb[:, nh * 512:(nh + 1) * 512], op0=ALU.mult, op1=ALU.add)
            mk.op("vector", "bn_stats", out=st6s[t % 2][:, nh, :], in_=z[:, nh * 512:(nh + 1) * 512])
        mk.op("vector", "bn_aggr", out=mvs[t % 2][:], in_=st6s[t % 2][:].rearrange("p a b -> p (a b)"))
        mk.op("vector", "tensor_scalar", out=rstds[t % 2][:], in0=mvs[t % 2][:, 1:2], scalar1=1e-5, scalar2=None, op0=ALU.add)

    def stage2(t):
        z = zs[t % 3]
        mv = mvs[t % 2]
        rstd = rstds[t % 2]
        mk.op("scalar", "activation", out=rstd[:], in_=rstd[:], func=AF.Sqrt)
        mk.op("vector", "reciprocal", out=rstd[:], in_=rstd[:])
        mk.op("vector", "tensor_scalar", out=mv[:, 1:2], in0=mv[:, 0:1], scalar1=rstd[:, 0:1], scalar2=-1.0, op0=ALU.mult, op1=ALU.mult)
        mk.op("scalar", "activation", out=z[:], in_=z[:], func=AF.Identity, scale=rstd[:, 0:1], bias=mv[:, 1:2])
        mk.op("vector", "tensor_tensor", out=z[:], in0=z[:], in1=gam[:], op=ALU.mult)
        mk.op("gpsimd", "tensor_tensor", out=z[:], in0=z[:], in1=bet[:], op=ALU.add)
        mk.dma("sync", out=o_d[t * 128:(t + 1) * 128, :], in_=z[:], key="outz%d" % (t % 3))
        if make_xT:
            mk.op("scalar", "activation", out=zb[:], in_=z[:], func=AF.Copy)
            bT = banks[6 + t % 2][:, :].bitcast(BF16)
            for c in range(8):
                mk.op("tensor", "transpose", out=bT[:, c * 128:(c + 1) * 128], in_=zb[:, c * 128:(c + 1) * 128], identity=identb[:])
            mk.op("scalar", "activation", out=xTt[:, t % 2].rearrange("p c q -> p (c q)"), in_=bT, func=AF.Copy)
            mk.dma("sync", out=xT_v[:, :, t * 128:(t + 1) * 128], in_=xTt[:, t % 2], key="xTo%d" % (t % 2))

    stage1(0)
    for t in range(NTB):
        if t + 1 < NTB:
            stage1(t + 1)
        stage2(t)
    mk.end_phase()
    es.close()


def build_F():
    nc = bass.Bass("TRN2", target_bir_lowering=False)

    def din(name, shape, dt=F32):
        return nc.dram_tensor(name, list(shape), dt, kind="ExternalInput").ap()

    xT0 = din("xT0", [DM, S]); x0 = din("x0", [S, DM])
    w_d = din("w", [4, DM, NCOL])
    cos_d = din("cos", [S, 32]); sin_d = din("sin", [S, 32])
    kd_d = din("kd", [2, 128, 2]); qd_d = din("qd", [2, 128, 128]); dt_d = din("dt", [2, 128, 256]); cdv_d = din("cdv", [2, 128, 1])
    mbig_d = din("mbig", [128, 2176]); negtri_d = din("negtri", [128, 512]); negtriw_d = din("negtriw", [128, 512])
    ident_d = din("ident", [128, 128]); selmap_d = din("selmap", [128, 130])
    cmb_d = din("cmb", [128, 126]); amb_d = din("amb", [128, 126]); esel_d = din("esel", [64, 4096])
    cmpw_d = din("cmpw", [2, 2, 2048, 64]); posT_d = din("posT", [2, 2, 64, 32])
    lruv_d = din("lruv", [4, 128, 9]); wa_d = din("wa", [4, 128, 128]); wx_d = din("wx", [4, 128, 128])
    wo_d = din("wo", [2, DM, DM]); g_d = din("lng", [2, 1, DM]); b_d = din("lnb", [2, 1, DM])
    xo = nc.dram_tensor("xo", [S, DM], F32, kind="ExternalOutput").ap()
    ytf = nc.dram_tensor("ytf_scr", [DM, S], BF16).ap()
    x1 = nc.dram_tensor("x1_scr", [S, DM], F32).ap()
    x1T = nc.dram_tensor("x1T_scr", [DM, S], BF16).ap()
    mk = MK(nc)
    banks = [nc.alloc_psum_tensor(f"bank{i}", [128, 512], F32) for i in range(8)]
    for l in range(DBG.get("nlayers", 2)):
        for hf in range(2):
            i4 = l * 2 + hf
            D = dict(xT=(xT0 if l == 0 else x1T), w=w_d[i4], cos=cos_d, sin=sin_d, kd=kd_d[hf], qd=qd_d[hf], dt=dt_d[hf],
                     cdv=cdv_d[hf], mbig=mbig_d, negtri=negtri_d, negtriw=negtriw_d, ident=ident_d, selmap=selmap_d,
                     cmb=cmb_d, amb=amb_d, esel=esel_d, cmpw=cmpw_d[l], posT=posT_d[l], lruv=lruv_d[i4], wa=wa_d[i4],
                     wx=wx_d[i4], yt=ytf[hf * 512:(hf + 1) * 512, :])
            emit_A(nc, mk, banks, "_a%d" % i4, D)
        last = (l == DBG.get("nlayers", 2) - 1)
        D = dict(ytf=ytf, xr=(x0 if l == 0 else x1), wo=wo_d[l], lng=g_d[l], lnb=b_d[l], xo=(xo if last else x1),
                 ident=ident_d, xTo=x1T)
        emit_B(nc, mk, banks, "_b%d" % l, D, make_xT=not last)
    return nc, mk


def f_inputs(inp, b):
    c0, c1 = make_consts(0), make_consts(1)
    m = {k: c0[k] for k in ("cos", "sin", "mbig", "negtri", "negtriw", "ident", "selmap", "cmb", "amb", "esel")}
    for k in ("kd", "qd", "dt", "cdv"):
        m[k] = np.ascontiguousarray(np.stack([c0[k], c1[k]]))
    xb = np.asarray(inp["x"][b], np.float32)
    m["x0"] = np.ascontiguousarray(xb)
    m["xT0"] = np.ascontiguousarray(xb.T)
    m["w"] = np.ascontiguousarray(np.stack([inp["w_in"][l][:, core_columns(hf)] for l in range(2) for hf in range(2)]))
    m["cmpw"] = np.ascontiguousarray(inp["nsa_cmp_w"])
    m["posT"] = np.ascontiguousarray(inp["nsa_cmp_pos"].transpose(0, 1, 3, 2))
    lruv = np.zeros((4, 128, 9), np.float32)
    wa = np.zeros((4, 128, 128), np.float32)
    wx = np.zeros((4, 128, 128), np.float32)
    for l in range(2):
        for hf in range(2):
            i4 = l * 2 + hf
            ch = slice(hf * 128, hf * 128 + 128)
            lruv[i4, :, 0:4] = inp["lru_conv_w"][l][:, ch].T
            lruv[i4, :, 4] = inp["lru_conv_b"][l][ch]
            lruv[i4, :, 5] = inp["lru_b_a"][l][ch]
            lruv[i4, :, 6] = inp["lru_b_x"][l][ch]
            lruv[i4, :, 7] = inp["lru_lambda"][l][ch]
            for j in range(2):
                wa[i4, j * 64:(j + 1) * 64, j * 64:(j + 1) * 64] = inp["lru_w_a"][l][2 * hf + j]
                wx[i4, j * 64:(j + 1) * 64, j * 64:(j + 1) * 64] = inp["lru_w_x"][l][2 * hf + j]
    m["lruv"] = lruv; m["wa"] = wa; m["wx"] = wx
    rows = np.concatenate([wout_rows(0), wout_rows(1)])
    m["wo"] = np.ascontiguousarray(np.stack([inp["w_out"][l][rows, :] for l in range(2)]))
    m["lng"] = np.ascontiguousarray(inp["ln_g"][:, None, :])
    m["lnb"] = np.ascontiguousarray(inp["ln_b"][:, None, :])
    return m


_PROG = {}


def kernel(**inputs):
    inp = {k: np.asarray(v) for k, v in inputs.items()}
    if "F" not in _PROG:
        _PROG["F"] = build_F()[0]
    in_maps = [f_inputs(inp, b) for b in range(4)]
    res = run_bass_kernel_spmd(_PROG["F"], in_maps, core_ids=list(range(4)))
    return np.stack([np.asarray(res.results[b]["xo"], dtype=np.float32) for b in range(4)])
```

```python
import math
from contextlib import ExitStack

import numpy as np
import ml_dtypes
import concourse.bass as bass
import concourse.mybir as mybir
from concourse.bass_utils import run_bass_kernel_spmd

F32 = mybir.dt.float32
BF16 = mybir.dt.bfloat16
ALU = mybir.AluOpType
AF = mybir.ActivationFunctionType

S = 4096
DM = 1024
NEG = -30000.0
DBG = {"nblocks": 8, "ret": True, "lru": True, "cmp": True, "nq": 32}

_AP_KW = ("in_", "in0", "in1", "lhsT", "rhs", "scalar", "scalar1", "scalar2", "bias", "scale",
          "data0", "data1", "initial", "in_to_replace", "in_values", "identity")
_ESZ = {F32: 4, BF16: 2}


def _is_ap(v):
    return hasattr(v, "tensor") and hasattr(v, "ap") and hasattr(v, "offset")


def _region(ap):
    t = ap.tensor
    name = t.name
    pat = list(ap.ap)
    esz = _ESZ.get(ap.dtype, 4)
    if "DRam" in type(t).__name__:
        lo = ap.offset
        hi = lo + sum((c - 1) * abs(s) for s, c in pat) + 1
        return (name, 0, 1, lo * esz, hi * esz, False)
    pstep, pcnt = pat[0]
    p0 = ap.start_partition()
    if pstep == 0:
        foff = ap.offset
    else:
        foff = ap.offset - p0 * pstep
    lo = foff
    hi = lo + sum((c - 1) * abs(s) for s, c in pat[1:]) + 1
    return (name, p0, p0 + pcnt, lo * esz, hi * esz, "PSum" in type(t).__name__)


class MK:
    ENG = ("tensor", "vector", "scalar", "gpsimd", "sync")

    def __init__(self, nc):
        self.nc = nc
        self.esem = {e: nc.alloc_semaphore("es_" + e) for e in self.ENG}
        self.nidx = {e: 0 for e in self.ENG}
        self.base = {e: 0 for e in self.ENG}
        self.dsem = {}
        self.dcount = {}
        self.n_inst = 0
        self.sfx = ""
        self.log = []
        self._reset()

    def _reset(self):
        self.streams = {e: [] for e in self.ENG}
        self.needed = {e: set() for e in self.ENG}
        self.known = {e: {} for e in self.ENG}
        self.acc = {}

    def _deps_and_record(self, tok, reads, writes):
        deps = {}
        rr = [_region(ap) for ap in reads]
        wr = [_region(ap) for ap in writes]
        for (name, p0, p1, lo, hi, ps) in rr:
            for r in self.acc.setdefault(name, []):
                if (ps and r[4] != tok[0]) or (r[6] and r[0] < p1 and p0 < r[1] and r[2] < hi and lo < r[3]):
                    if deps.get(r[4], -1) < r[5]:
                        deps[r[4]] = r[5]
        for (name, p0, p1, lo, hi, ps) in wr:
            for r in self.acc.setdefault(name, []):
                if (ps and r[4] != tok[0]) or (r[0] < p1 and p0 < r[1] and r[2] < hi and lo < r[3]):
                    if deps.get(r[4], -1) < r[5]:
                        deps[r[4]] = r[5]
        for (name, p0, p1, lo, hi, ps) in rr + wr:
            if ps:
                recs = self.acc[name]
                recs[:] = [r for r in recs if r[4] == tok[0]]
        for (name, p0, p1, lo, hi, ps) in rr:
            recs = self.acc[name]
            recs[:] = [r for r in recs if not ((not r[6]) and r[4] == tok[0] and p0 <= r[0] and r[1] <= p1
                                               and lo <= r[2] and r[3] <= hi)]
            recs.append((p0, p1, lo, hi, tok[0], tok[1], False))
        for (name, p0, p1, lo, hi, ps) in wr:
            recs = self.acc[name]
            recs[:] = [r for r in recs if not (p0 <= r[0] and r[1] <= p1 and lo <= r[2] and r[3] <= hi)]
            recs.append((p0, p1, lo, hi, tok[0], tok[1], True))
        return deps

    def _emit_waits(self, eng, deps):
        kn = self.known[eng]
        self._lw = []
        for k, v in deps.items():
            if k.startswith("d:"):
                v = self.dcount[k]
            if eng == "tensor" and k == "e:tensor":
                continue
            if kn.get(k, -1) >= v:
                continue
            kn[k] = v
            self._lw.append(k)
            if k.startswith("e:"):
                self.needed[k[2:]].add(v)
            self.streams[eng].append(("wait", k, v))

    def op(self, eng, method, **kw):
        if DBG.get("limit") is not None and self.n_inst >= DBG["limit"]:
            return
        writes = [kw["out"]] if "out" in kw else []
        if kw.get("accum_out", None) is not None:
            writes.append(kw["accum_out"])
        if method == "memset":
            writes.append(kw["ap"])
        reads = [kw[k] for k in _AP_KW if k in kw and _is_ap(kw[k])]
        reads += list(kw.pop("_reads", ()))
        writes += list(kw.pop("_writes", ()))
        idx = self.nidx[eng]
        self.nidx[eng] += 1
        deps = self._deps_and_record(("e:" + eng, idx), reads, writes)
        self._emit_waits(eng, deps)
        self.streams[eng].append(("op", method, kw, idx))
        if DBG.get("log"):
            o = kw.get("out", kw.get("ap"))
            self.log.append((eng, method, o.tensor.name if o is not None else "", tuple(self._lw)))
        self.n_inst += 1

    def dma(self, eng, out, in_, key=None, **kw):
        if DBG.get("limit") is not None and self.n_inst >= DBG["limit"] and not kw.pop("_force", False):
            return
        kw.pop("_force", None)
        if key is None:
            key = out.tensor.name if "DRam" not in type(out.tensor).__name__ else in_.tensor.name
            if self.sfx and key.endswith(self.sfx):
                key = key[:-len(self.sfx)]
        key = "d:" + key
        if key not in self.dsem:
            self.dsem[key] = self.nc.alloc_semaphore("ds_" + key[2:])
            self.dcount[key] = 0
        val = self.dcount[key] + 16
        deps = self._deps_and_record((key, val), [in_], [out])
        self._emit_waits(eng, deps)
        self.dcount[key] = val
        self.streams[eng].append(("dma", out, in_, key, kw))
        self.n_inst += 1

    def coll(self, kind, rg, in_, out, key="cc"):
        eng = "gpsimd"
        key = "d:" + key
        if key not in self.dsem:
            self.dsem[key] = self.nc.alloc_semaphore("ds_" + key[2:])
            self.dcount[key] = 0
        val = self.dcount[key] + 16
        deps = self._deps_and_record((key, val), [in_], [out])
        self._emit_waits(eng, deps)
        self.dcount[key] = val
        self.streams[eng].append(("coll", kind, rg, in_, out, key))
        self.n_inst += 1

    def end_phase(self):
        for k, v in self.dcount.items():
            if self.known["sync"].get(k, -1) < v:
                self.streams["sync"].append(("wait", k, v))
        valmap = {}
        for e in self.ENG:
            s = sorted(self.needed[e])
            valmap[e] = {idx: self.base[e] + i + 1 for i, idx in enumerate(s)}

        def run_stream(e, engobj):
            vm = valmap[e]
            for item in self.streams[e]:
                if item[0] == "wait":
                    _, k, v = item
                    if k.startswith("e:"):
                        engobj.wait_ge(self.esem[k[2:]], valmap[k[2:]][v])
                    else:
                        engobj.wait_ge(self.dsem[k], v)
                elif item[0] == "op":
                    _, method, kw, idx = item
                    inst = getattr(engobj, method)(**kw)
                    if idx in vm:
                        inst.then_inc(self.esem[e], 1)
                elif item[0] == "coll":
                    _, kind, rg, in_, out, key = item
                    engobj.collective_compute(kind, ALU.bypass, replica_groups=rg, ins=[in_], outs=[out]).then_inc(self.dsem[key], 16)
                else:
                    _, out, in_, key, kw = item
                    engobj.dma_start(out=out, in_=in_, **kw).then_inc(self.dsem[key], 16)

        with self.nc.Block() as blk:
            for e in self.ENG:
                if not self.streams[e]:
                    continue
                getattr(blk, e)(lambda engobj, e=e: run_stream(e, engobj))
        for e in self.ENG:
            self.base[e] += len(self.needed[e])
        self._reset()


OFF = dict(rq=0, rk=256, rv=512, rg=768, nq=1024, nkc=1536, nvc=1664, nks=1792, nvs=1920, nkw=2048,
           nvw=2176, ng=2304, ngl=2816, lx=2840, lg=3096)
NFM = 8
GLC = NFM * 128
TM0 = GLC + 12
NCOL = TM0 + 640


def core_columns(hf):
    r = np.arange
    g = hf
    cols = []
    cols += list(OFF["nq"] + g * 256 + r(128))
    cols += list(OFF["nq"] + g * 256 + 128 + r(128))
    cols += list(OFF["nks"] + g * 64 + r(64)) + list(OFF["nkw"] + g * 64 + r(64))
    cols += list(OFF["nkc"] + g * 64 + r(64)) + list(OFF["nvc"] + g * 64 + r(64))
    cols += list(OFF["ng"] + g * 256 + r(128))
    cols += list(OFF["ng"] + g * 256 + 128 + r(128))
    cols += list(OFF["lx"] + hf * 128 + r(128))
    cols += list(OFF["lg"] + hf * 128 + r(128))
    cols += list(OFF["ngl"] + g * 12 + r(12))
    cols += list(OFF["nvs"] + g * 64 + r(64)) + list(OFF["nvw"] + g * 64 + r(64))
    for nm in ("rq", "rk", "rv", "rg"):
        cols += list(OFF[nm] + hf * 128 + r(128))
    assert len(cols) == NCOL
    return np.array(cols)


def wout_rows(hf):
    r = np.arange
    return np.concatenate([hf * 128 + r(128), 256 + hf * 256 + r(256), 768 + hf * 128 + r(128)])


_CONST_CACHE = {}


def make_consts(hf):
    if hf in _CONST_CACHE:
        return _CONST_CACHE[hf]
    c = {}
    half = 32
    inv = (1.0 / (10000.0 ** (np.arange(half, dtype=np.float32) / half))).astype(np.float32)
    ang = np.arange(S, dtype=np.float32)[:, None] * inv[None, :]
    c["cos"] = np.cos(ang).astype(np.float32)
    c["sin"] = np.sin(ang).astype(np.float32)
    hs = np.array([2 * hf, 2 * hf + 1], dtype=np.float32)
    log_g = np.log(1.0 - 2.0 ** (-5.0 - hs)).astype(np.float32)
    idx = np.arange(128, dtype=np.float32)
    kd = np.exp((127.0 - idx)[:, None] * log_g[None, :]) / 8.0
    c["kd"] = kd.astype(np.float32)
    qd = np.exp((idx + 1.0)[None, :] * log_g[:, None])
    c["qd"] = np.repeat(qd, 64, axis=0).astype(np.float32)
    diff = idx[None, :] - idx[:, None]
    dt = np.where(diff[None] >= 0, np.exp(np.maximum(diff, 0.0)[None] * log_g[:, None, None]), 0.0) / 8.0
    c["dt"] = np.ascontiguousarray(dt.transpose(1, 0, 2)).reshape(128, 256).astype(np.float32)
    c["cdv"] = np.repeat(np.exp(128.0 * log_g), 64)[:, None].astype(np.float32)
    n = np.arange(128)[:, None]
    cc = np.arange(2176)[None, :]
    c["mbig"] = (cc >= 16 * n + 31).astype(np.float32)
    k = np.arange(128)[:, None]
    q = np.arange(128)[None, :]
    c["negtri"] = np.tile(np.where(k > q, NEG, 0.0), (1, 4)).astype(np.float32)
    c["negtriw"] = np.tile(np.where(k <= q, NEG, 0.0), (1, 4)).astype(np.float32)
    c["ident"] = np.eye(128, dtype=np.float32)
    ci = np.arange(256)[:, None]
    sj = np.arange(64)[None, :]
    ov = np.minimum(ci * 16 + 32, sj * 64 + 64) - np.maximum(ci * 16, sj * 64)
    sm = np.clip(ov, 0, None) / 16.0
    sm[255] = 0.0
    sma = np.concatenate([sm, np.ones((256, 1))], axis=1)
    sma[255] = 0.0
    c["selmap"] = np.ascontiguousarray(sma.reshape(2, 128, 65).transpose(1, 0, 2)).reshape(128, 130).astype(np.float32)
    rel = np.arange(126)[None, :] - 62
    relcur = (np.arange(128)[:, None] >= 64).astype(np.int64)
    cm = (rel < relcur - 1).astype(np.float32)
    am = np.where(rel == relcur, 2.0e4, np.where(rel == relcur - 1, 1.0e4, np.where(rel > relcur, -1.0e4, 0.0)))
    jj = np.arange(64)[:, None]
    kk = np.arange(4096)[None, :]
    c["esel"] = (jj == kk // 64).astype(np.float32)
    c["cmb"] = cm.astype(np.float32)
    c["amb"] = am.astype(np.float32)
    _CONST_CACHE[hf] = c
    return c


def emit_A(nc, mk, banks, sfx, D):
    xT_d = D["xT"]; w_d = D["w"]
    cos_d = D["cos"]; sin_d = D["sin"]
    kd_d = D["kd"]; qd_d = D["qd"]; dt_d = D["dt"]
    mbig_d = D["mbig"]; negtri_d = D["negtri"]; negtriw_d = D["negtriw"]
    ident_d = D["ident"]; selmap_d = D["selmap"]
    cmb_d = D["cmb"]; amb_d = D["amb"]
    cmpw_d = D["cmpw"]; posT_d = D["posT"]
    lruv_d = D["lruv"]
    wa_d = D["wa"]; wx_d = D["wx"]
    cdv_d = D["cdv"]
    esel_d = D["esel"]
    yt_d = D["yt"]
    mk.sfx = sfx
    pes = ExitStack()

    def A(name, shape, dt):
        return pes.enter_context(nc.sbuf_tensor(name + sfx, list(shape), dt))

    qT = A("qT", [128, 32, 2, 128], BF16)
    ksT = A("ksT", [128, S], BF16)
    kwT = A("kwT", [128, S], BF16)
    kvcT = A("kvcT", [128, S], BF16)
    Vs = A("Vs", [128, 32, 128], BF16)
    Vw = A("Vw", [128, 32, 128], BF16)
    ngs = A("ngs", [128, 2, S], BF16)
    gl = A("gl", [32, S], BF16)
    YT = A("YT", [128, 4, S], BF16)
    mbig = A("mbig_sb", [128, 2176], BF16)
    identb = A("identb", [128, 128], BF16)
    identf = A("identf", [128, 128], F32)
    negtri = A("negtri_sb", [128, 512], BF16)
    negtriw = A("negtriw_sb", [128, 512], BF16)
    selmap = A("selmap_sb", [128, 130], BF16)
    cmb = A("cmb_sb", [128, 126], F32)
    amb = A("amb_sb", [128, 126], F32)
    kcT = A("kcT", [128, 256], BF16)
    Vc = A("Vc", [128, 2, 128], BF16)
    tiny = A("tiny", [128, 1], F32)

    def V(method, **kw):
        mk.op("vector", method, **kw)

    def ACT(method="activation", **kw):
        mk.op("scalar", method, **kw)

    def PE(method="matmul", **kw):
        mk.op("tensor", method, **kw)

    def POOL(method, **kw):
        mk.op("gpsimd", method, **kw)

    es = ExitStack()

    def T(name, shape, dt):
        return es.enter_context(nc.sbuf_tensor(name + sfx, list(shape), dt))

    w_sb = T("w_sb", [128, 8, NCOL], BF16)
    xt = T("xt", [128, 2, 8, 512], BF16)
    cos_sb = T("cos_sb", [128, 32, 32], F32)
    sin_sb = T("sin_sb", [128, 32, 32], F32)
    kd_sb = T("kd_sb", [128, 2], F32)
    qd_sb = T("qd_sb", [128, 128], F32)
    dt_sb = T("dt_sb", [128, 256], F32)
    Wk = T("Wk", [128, 32, 64], BF16)
    posT = T("posT_sb", [128, 32], BF16)
    lruv = T("lruv_sb", [128, 9], F32)
    cdv = T("cdv_sb", [128, 1], F32)
    wa_sb = T("wa_sb", [128, 128], BF16)
    wx_sb = T("wx_sb", [128, 128], BF16)
    qk_sb = T("qk_sb", [128, 2, 256], F32)
    rt = [T(f"rt{i}", [128, 128], F32) for i in range(2)]
    rot2 = T("rot", [128, 2, 256], BF16)
    v_sb = T("v_sb", [128, 2, 128], BF16)
    gs_sb = T("gs_sb", [128, 2, 128], F32)
    kdec2 = T("kdec", [128, 2, 128], BF16)
    qrT = T("qrT", [128, 256], BF16)
    qdT = T("qdT", [128, 128], BF16)
    krT = T("krT", [128, 128], BF16)
    scT_sb = T("scT_sb", [128, 256], BF16)
    state_f = T("state_f", [128, 128], F32)
    state_bf = T("state_bf", [128, 2, 128], BF16)
    st6 = T("st6", [128, 2, 6], F32)
    mv = T("mv", [128, 2, 2], F32)
    rstd = T("rstd", [128, 2], F32)
    yn = T("yn", [128, 128], F32)
    yg = T("yg", [128, 128], BF16)
    lxb = T("lxb", [128, 2, 515], F32)
    xc = T("xc", [128, 512], F32)
    xcb = T("xcb", [128, 512], BF16)
    r_sb = T("r_sb", [128, 512], F32)
    i_sb = T("i_sb", [128, 512], F32)
    a_sb = T("a_sb", [128, 512], F32)
    u_sb = T("u_sb", [128, 512], F32)
    h_sb = T("h_sb", [128, 512], F32)
    hlast = T("hlast", [128, 1], F32)
    lgs = T("lgs", [128, 512], BF16)
    sp = T("sp", [128, 4], F32)
    biask = T("biask", [128, 1], F32)
    biasv = T("biasv", [128, 64], BF16)
    one_f = T("one_f", [128, 1], F32)

    for c in range(8):
        mk.dma("gpsimd", out=w_sb[:, c, :], in_=w_d[c * 128:(c + 1) * 128, :])
    mk.dma("sync", out=cos_sb[:], in_=cos_d.rearrange("(c p) f -> p c f", p=128))
    mk.dma("sync", out=sin_sb[:], in_=sin_d.rearrange("(c p) f -> p c f", p=128))
    mk.dma("sync", out=kd_sb[:], in_=kd_d)
    mk.dma("sync", out=qd_sb[:], in_=qd_d)
    mk.dma("sync", out=dt_sb[:], in_=dt_d)
    mk.dma("sync", out=identf[:], in_=ident_d)
    mk.dma("sync", out=cmb[:], in_=cmb_d)
    mk.dma("sync", out=amb[:], in_=amb_d)
    mk.dma("sync", out=lruv[:], in_=lruv_d)
    mk.dma("sync", out=cdv[:], in_=cdv_d)
    mk.dma("gpsimd", out=mbig[:, 0:1088], in_=mbig_d[:, 0:1088])
    mk.dma("gpsimd", out=mbig[:, 1088:2176], in_=mbig_d[:, 1088:2176])
    mk.dma("gpsimd", out=identb[:], in_=ident_d)
    mk.dma("gpsimd", out=negtri[:], in_=negtri_d)
    mk.dma("gpsimd", out=negtriw[:], in_=negtriw_d)
    mk.dma("gpsimd", out=selmap[:], in_=selmap_d)
    mk.dma("gpsimd", out=wa_sb[:], in_=wa_d)
    mk.dma("gpsimd", out=wx_sb[:], in_=wx_d)
    mk.dma("gpsimd", out=Wk[0:64, :, :], in_=cmpw_d[0].rearrange("(l d) e -> d l e", d=64))
    mk.dma("gpsimd", out=Wk[64:128, :, :], in_=cmpw_d[1].rearrange("(l d) e -> d l e", d=64))
    mk.dma("gpsimd", out=posT[0:64, :], in_=posT_d[0])
    mk.dma("gpsimd", out=posT[64:128, :], in_=posT_d[1])
    V("memset", ap=tiny[0:64, :], constant=0.0)
    V("memset", ap=tiny[64:128, :], constant=1e-18)
    V("memset", ap=qrT[:], constant=0.0)
    V("memset", ap=state_f[:], constant=0.0)
    V("memset", ap=one_f[:], constant=1.0)
    V("memset", ap=Vs[:, :, 64:128], constant=1.0)
    V("memset", ap=Vw[:, :, 64:128], constant=1.0)
    V("memset", ap=Vc[:], constant=0.0)
    V("memset", ap=kcT[:], constant=0.0)
    V("memset", ap=kwT[64:128, :], constant=0.0)
    mk.dma("gpsimd", out=ksT[64:128, 0:2048], in_=esel_d[:, 0:2048], key="esel")
    mk.dma("gpsimd", out=ksT[64:128, 2048:4096], in_=esel_d[:, 2048:4096], key="esel")
    V("memset", ap=lxb[:, 1, 512:515], constant=0.0)
    ACT(out=sp[:, 0:1], in_=lruv[:, 7:8], func=AF.Exp, scale=-1.0)
    V("tensor_scalar", out=sp[:, 0:1], in0=sp[:, 0:1], scalar1=1.0, scalar2=None, op0=ALU.add)
    ACT(out=sp[:, 1:2], in_=sp[:, 0:1], func=AF.Ln)
    V("tensor_scalar", out=sp[:, 2:3], in0=sp[:, 1:2], scalar1=-8.0, scalar2=None, op0=ALU.mult)
    V("tensor_scalar", out=sp[:, 3:4], in0=sp[:, 1:2], scalar1=-16.0, scalar2=None, op0=ALU.mult)

    xT_v = xT_d.rearrange("(c p) t -> p c t", p=128)
    fm_bank = [banks[0], banks[1]]
    fmi = 0
    psA = banks[2]
    psB = banks[3]
    bankT = banks[4]
    tq_ps = bankT[:, 0:64].bitcast(BF16)
    tk_ps = bankT[:, 64:128].bitcast(BF16)
    ty_ps = banks[7][:, 0:64].bitcast(BF16)
    scT_ps = banks[5][:, 0:256]
    y_ps = banks[5][:, 256:384]
    kv_ps = banks[5][:, 384:512]
    ps_r = banks[6]
    ps_i = banks[7]

    NB = DBG["nblocks"]

    def xload(tbn):
        xbn = xt[:, tbn % 2]
        mk.dma("gpsimd", out=xbn[:, 0:4, :], in_=xT_v[:, 0:4, tbn * 512:tbn * 512 + 512], key="xt%d" % (tbn % 2))
        mk.dma("gpsimd", out=xbn[:, 4:8, :], in_=xT_v[:, 4:8, tbn * 512:tbn * 512 + 512], key="xt%d" % (tbn % 2))

    fmc = [0]

    def G(tb, g):
        xb = xt[:, tb % 2]
        t0 = tb * 512
        M = 128 if g < NFM else 12
        ps = fm_bank[fmc[0] % 2]
        fmc[0] += 1
        for c in range(8):
            PE(out=ps[0:M, :], lhsT=w_sb[:, c, g * 128:g * 128 + M], rhs=xb[:, c, :], start=(c == 0), stop=(c == 7))
        if g in (0, 1):
            V("tensor_copy", out=qT[:, tb * 4:(tb + 1) * 4, g, :], in_=ps[:].rearrange("p (a q) -> p a q", a=4))
        elif g == 2:
            ACT(out=ksT[0:64, t0:t0 + 512], in_=ps[0:64, :], func=AF.Copy)
            V("tensor_copy", out=kwT[0:64, t0:t0 + 512], in_=ps[64:128, :])
        elif g == 3:
            ACT(out=kvcT[:, t0:t0 + 512], in_=ps[:], func=AF.Copy)
        elif g in (4, 5):
            ACT(out=ngs[:, g - 4, t0:t0 + 512], in_=ps[:], func=AF.Silu)
        elif g == 6:
            V("tensor_copy", out=lxb[:, tb % 2, 3:515], in_=ps[:])
        elif g == 7:
            ACT(out=lgs[:], in_=ps[:], func=AF.Silu)
        else:
            ACT(out=gl[0:12, t0:t0 + 512], in_=ps[0:12, :], func=AF.Sigmoid)

    def TM(ch):
        tb, tt = ch // 4, ch % 4
        xb = xt[:, tb % 2]
        for c in range(8):
            PE(out=psB[:, 0:128], lhsT=xb[:, c, tt * 128:(tt + 1) * 128], rhs=w_sb[:, c, TM0:TM0 + 128],
               start=(c == 0), stop=(c == 7))
        for c in range(8):
            PE(out=psA[:], lhsT=xb[:, c, tt * 128:(tt + 1) * 128], rhs=w_sb[:, c, TM0 + 128:TM0 + 640],
               start=(c == 0), stop=(c == 7))
        V("tensor_copy", out=Vs[:, ch, 0:64], in_=psB[:, 0:64])
        V("tensor_copy", out=Vw[:, ch, 0:64], in_=psB[:, 64:128])
        ACT(out=qk_sb[:, ch % 2, :], in_=psA[:, 0:256], func=AF.Copy)
        ACT(out=v_sb[:, ch % 2, :], in_=psA[:, 256:384], func=AF.Copy)
        ACT(out=gs_sb[:, ch % 2, :], in_=psA[:, 384:512], func=AF.Silu)

    def R0(ch):
        qs4 = qk_sb[:, ch % 2, :].rearrange("p (a b c) -> p a b c", a=4, b=2)
        rot = rot2[:, ch % 2, :]
        kdec = kdec2[:, ch % 2, :]
        rot4 = rot.rearrange("p (a b c) -> p a b c", a=4, b=2)
        cosb = cos_sb[:, ch, :].unsqueeze(1).to_broadcast([128, 4, 32])
        sinb = sin_sb[:, ch, :].unsqueeze(1).to_broadcast([128, 4, 32])
        r0 = rt[0][:].rearrange("p (a c) -> p a c", a=4)
        r1 = rt[1][:].rearrange("p (a c) -> p a c", a=4)
        V("tensor_tensor", out=r0, in0=qs4[:, :, 0, :], in1=cosb, op=ALU.mult)
        V("tensor_tensor", out=r1, in0=qs4[:, :, 1, :], in1=sinb, op=ALU.mult)
        V("tensor_tensor", out=rot4[:, :, 0, :], in0=r0, in1=r1, op=ALU.subtract)
        V("tensor_tensor", out=r0, in0=qs4[:, :, 0, :], in1=sinb, op=ALU.mult)
        V("tensor_tensor", out=r1, in0=qs4[:, :, 1, :], in1=cosb, op=ALU.mult)
        V("tensor_tensor", out=rot4[:, :, 1, :], in0=r0, in1=r1, op=ALU.add)
        POOL("tensor_tensor", out=kdec.rearrange("p (h d) -> p h d", h=2),
             in0=rot[:, 128:256].rearrange("p (h d) -> p h d", h=2),
             in1=kd_sb[:].unsqueeze(2).to_broadcast([128, 2, 64]), op=ALU.mult)

    def R1(ch):
        PE("transpose", out=tq_ps, in_=rot2[:, ch % 2, 0:128], identity=identb[:])
        PE("transpose", out=tk_ps, in_=rot2[:, ch % 2, 128:256], identity=identb[:])
        ACT(out=qrT[0:64, 0:128], in_=tq_ps[0:64, :], func=AF.Copy)
        ACT(out=qrT[64:128, 128:256], in_=tq_ps[64:128, :], func=AF.Copy)
        V("tensor_tensor", out=qdT[:], in0=tq_ps, in1=qd_sb[:], op=ALU.mult)
        ACT(out=krT[:], in_=tk_ps, func=AF.Copy)

    def R2(ch):
        PE(out=scT_ps, lhsT=krT[:], rhs=qrT[:], start=True, stop=True)
        V("tensor_tensor", out=scT_sb[:], in0=scT_ps, in1=dt_sb[:], op=ALU.mult)

    def R3(ch):
        vv = v_sb[:, ch % 2, :]
        for h in range(2):
            PE(out=y_ps[:, h * 64:(h + 1) * 64], lhsT=scT_sb[:, h * 128:(h + 1) * 128], rhs=vv[:, h * 64:(h + 1) * 64],
               start=(h == 0), stop=(ch == 0 and h == 1), skip_group_check=True)
        if ch > 0:
            PE(out=y_ps, lhsT=qdT[:], rhs=state_bf[:, ch % 2, :], start=False, stop=True, skip_group_check=True)
        PE(out=kv_ps, lhsT=kdec2[:, ch % 2, :], rhs=vv, start=True, stop=True)
        for h in range(2):
            V("bn_stats", out=st6[:, h, :], in_=y_ps[:, h * 64:(h + 1) * 64])
            V("bn_aggr", out=mv[:, h, :], in_=st6[:, h, :])
        V("tensor_scalar", out=rstd[:], in0=mv[:, :, 1], scalar1=1e-5, scalar2=None, op0=ALU.add)
        ACT(out=rstd[:], in_=rstd[:], func=AF.Sqrt)
        V("reciprocal", out=rstd[:], in_=rstd[:])
        for h in range(2):
            V("tensor_scalar", out=yn[:, h * 64:(h + 1) * 64], in0=y_ps[:, h * 64:(h + 1) * 64],
              scalar1=mv[:, h, 0:1], scalar2=rstd[:, h:h + 1], op0=ALU.subtract, op1=ALU.mult)
        for h in range(2):
            hp = slice(64 * h, 64 * h + 64)
            hc = slice(64 * h, 64 * h + 64)
            if ch == 0:
                V("tensor_copy", out=state_f[hp, hc], in_=kv_ps[hp, 64 * h:64 * h + 64])
            else:
                V("scalar_tensor_tensor", out=state_f[hp, hc], in0=state_f[hp, hc], scalar=cdv[hp, 0:1],
                  in1=kv_ps[hp, 64 * h:64 * h + 64], op0=ALU.mult, op1=ALU.add)
        POOL("tensor_copy", out=state_bf[:, (ch + 1) % 2, :], in_=state_f[:])
        V("tensor_tensor", out=yg[:], in0=yn[:], in1=gs_sb[:, ch % 2, :], op=ALU.mult)

    def R4(ch):
        PE("transpose", out=ty_ps, in_=yg[:], identity=identb[:])
        ACT(out=YT[:, 0, ch * 128:(ch + 1) * 128], in_=ty_ps, func=AF.Copy)

    def LRU_a(tb):
        lx = lxb[:, tb % 2]
        if tb == 0:
            V("memset", ap=lx[:, 0:3], constant=0.0)
        else:
            V("tensor_copy", out=lx[:, 0:3], in_=lxb[:, (tb - 1) % 2, 512:515])
        V("tensor_scalar", out=xc[:], in0=lx[:, 0:512], scalar1=lruv[:, 0:1], scalar2=lruv[:, 4:5], op0=ALU.mult, op1=ALU.add)
        for w in range(1, 4):
            V("scalar_tensor_tensor", out=xc[:], in0=lx[:, w:w + 512], scalar=lruv[:, w:w + 1], in1=xc[:],
              op0=ALU.mult, op1=ALU.add)
        V("tensor_copy", out=xcb[:], in_=xc[:])

    def LRU_b(tb):
        t0 = tb * 512
        PE(out=ps_r[:], lhsT=wa_sb[:], rhs=xcb[:], start=True, stop=True)
        PE(out=ps_i[:], lhsT=wx_sb[:], rhs=xcb[:], start=True, stop=True)
        ACT(out=r_sb[:], in_=ps_r[:], func=AF.Sigmoid, bias=lruv[:, 5:6])
        ACT(out=i_sb[:], in_=ps_i[:], func=AF.Sigmoid, bias=lruv[:, 6:7])
        ACT(out=a_sb[:], in_=r_sb[:], func=AF.Exp, scale=sp[:, 2:3])
        ACT(out=r_sb[:], in_=r_sb[:], func=AF.Exp, scale=sp[:, 3:4])
        ACT(out=r_sb[:], in_=r_sb[:], func=AF.Sqrt, scale=-1.0, bias=one_f[:])
        POOL("tensor_tensor", out=u_sb[:], in0=i_sb[:], in1=xc[:], op=ALU.mult)
        V("tensor_tensor", out=u_sb[:], in0=u_sb[:], in1=r_sb[:], op=ALU.mult)
        init = 0.0 if tb == 0 else hlast[:, 0:1]
        V("tensor_tensor_scan", out=h_sb[:], data0=a_sb[:], data1=u_sb[:], initial=init, op0=ALU.mult, op1=ALU.add)
        V("tensor_copy", out=hlast[:], in_=h_sb[:, 511:512])
        V("tensor_tensor", out=YT[:, 3, t0:t0 + 512], in0=h_sb[:], in1=lgs[:], op=ALU.mult)

    FILL = [[0, 1, 2], [3, 4], [5, 6], [7, 8]]
    NCH = NB * 4
    for ch in range(NCH):
        tb, tt = ch // 4, ch % 4
        if tt == 0:
            for tbn in ([0, 1] if tb == 0 else [tb + 1]):
                if tbn < NB:
                    xload(tbn)
        fl = list(FILL[tt])
        TM(ch)
        if DBG["ret"]:
            R0(ch)
        if ch > 1 and DBG["ret"]:
            R4(ch - 2)
        if ch > 0 and DBG["ret"]:
            R1(ch - 1)
        G(tb, fl.pop(0))
        if ch > 0 and DBG["ret"]:
            R2(ch - 1)
        G(tb, fl.pop(0))
        if ch > 0 and DBG["ret"]:
            R3(ch - 1)
        if fl:
            G(tb, fl.pop(0))
        if tt == 3 and DBG["lru"]:
            LRU_a(tb)
        if tt == 0 and tb > 0 and DBG["lru"]:
            LRU_b(tb - 1)
    if DBG["lru"] and NB > 0:
        LRU_b(NB - 1)
    if DBG["ret"] and NCH > 0:
        if NCH > 1:
            R4(NCH - 2)
        R1(NCH - 1); R2(NCH - 1); R3(NCH - 1); R4(NCH - 1)

    cps = banks[0]
    for l in range(32 if DBG["cmp"] else 0):
        PE(out=cps[0:64, 0:255], lhsT=Wk[0:64, l, :],
           rhs=kvcT[0:64, l:l + 16 * 254 + 1:16], start=(l == 0), stop=(l == 31))
    bps = banks[1]
    for l in range(32 if DBG["cmp"] else 0):
        PE(out=bps[0:64, 0:1], lhsT=Wk[0:64, l, :],
           rhs=posT[0:64, l:l + 1], start=(l == 0), stop=(l == 31))
    if DBG["cmp"]:
        V("tensor_copy", out=biask[0:64, :], in_=bps[0:64, 0:1])
        ACT(out=kcT[0:64, 0:255], in_=cps[0:64, 0:255], func=AF.Identity, bias=biask[0:64, :])
    bvp = banks[2]
    for l in range(32 if DBG["cmp"] else 0):
        PE(out=bvp[0:128, 0:64], lhsT=posT[64:128, l:l + 1].to_broadcast([64, 128]),
           rhs=Wk[64:128, l, :], start=(l == 0), stop=(l == 31))
    if DBG["cmp"]:
        V("tensor_copy", out=biasv[:], in_=bvp[:, 0:64])
    for nt in range(2 if DBG["cmp"] else 0):
        nn = 128 if nt == 0 else 127
        vps = banks[3 + nt]
        for l in range(32):
            s0 = nt * 2048 + l
            PE(out=vps[0:nn, 0:64], lhsT=kvcT[64:128, s0:s0 + 16 * (nn - 1) + 1:16], rhs=Wk[64:128, l, :],
               start=(l == 0), stop=(l == 31))
        V("tensor_tensor", out=Vc[0:nn, nt, 0:64], in0=vps[0:nn, 0:64], in1=biasv[0:nn, :], op=ALU.add)
        V("memset", ap=Vc[0:nn, nt, 64:128], constant=1.0)

    mk.end_phase()
    es.close()

    es = ExitStack()
    PT = [T(f"PT{i}", [128, 512], BF16) for i in range(4)]
    PTm = T("PTm", [128, 512], BF16)
    Osb = [[T(f"Osb{i}_{j}", [128, 512], F32) for j in range(2)] for i in range(3)]
    PTmm = [T(f"PTm{j}", [128, 512], BF16) for j in range(2)]
    c_sbs = [T(f"c_sb{j}", [64, 512], F32) for j in range(3)]
    y_sb = T("y_sb", [128, 256], F32)
    tmp_sbs = [T(f"tmp_sb{j}", [128, 256], F32) for j in range(3)]
    A_sb = T("A_sb", [128, 260], F32)
    rdc = T("rdc", [128, 4], F32)
    imp = T("imp", [128, 64], F32)
    imp2 = T("imp2", [128, 64], F32)
    m8 = T("m8", [128, 16], F32)
    nsel = T("nsel", [128, 64], F32)
    qn = T("qn", [128, 2, 512], BF16)
    POOL("memset", ap=qn[:], constant=0.0)

    ST = [banks[0], banks[1]]
    OsB = [banks[2], banks[2]]
    OcwB = banks[3]
    gbBs = [banks[4], banks[5], banks[6]]
    A_ps = banks[7][:, 0:260]
    nT_ps = banks[7][0:64, 384:512]
    sti = [0]
    pti = [0]

    def make_qbd(i):
        qb = qn[0:64, i % 2, :].rearrange("p (j r q) -> p j r q", j=2, r=2)
        POOL("tensor_copy", out=qb[:, :, 0, :], in_=qT[0:64, i, :, :])
        POOL("tensor_copy", out=qb[:, :, 1, :], in_=qT[64:128, i, :, :])

    def score_tile(kT_sb, k0, i, mask=None, dst=None):
        st = ST[sti[0] % 2]
        sti[0] += 1
        PE(out=st[:], lhsT=kT_sb[:, k0:k0 + 128], rhs=qn[:, i % 2, :], start=True, stop=(mask is None), skip_group_check=True)
        if mask is not None:
            lhsT, rhs = mask
            PE(out=st[:], lhsT=lhsT, rhs=rhs, start=False, stop=True, skip_group_check=True)
        if dst is None:
            pt = PT[pti[0] % 4]
            pti[0] += 1
        else:
            pt = dst
        ACT(out=pt[:], in_=st[:], func=AF.Exp, scale=0.125)
        return pt

    def bc4(ap2d, p):
        return ap2d.unsqueeze(1).to_broadcast([p, 4, 128])

    cmp_ptm = {}

    def cmp_A(i):
        make_qbd(i)
        nts = [0] if i < 16 else [0, 1]
        lst = []
        for nt in nts:
            if nt == 0 and i >= 17:
                ptm = score_tile(kcT, nt * 128, i, dst=PTmm[nt])
            else:
                pt = score_tile(kcT, nt * 128, i)
                c0 = 128 * i if nt == 0 else 128 * (i - 16)
                ptm = PTmm[nt]
                V("tensor_tensor", out=ptm[:].rearrange("p (h q) -> p h q", h=4), in0=pt[:].rearrange("p (h q) -> p h q", h=4),
                  in1=bc4(mbig[:, c0:c0 + 128], 128), op=ALU.mult)
            lst.append(ptm)
        cmp_ptm[i] = lst

    def cmp_B(i):
        nts = [0] if i < 16 else [0, 1]
        for nt in nts:
            ptm = cmp_ptm[i][nt]
            PE(out=OcwB[:], lhsT=Vc[:, nt, :], rhs=ptm[:], start=(nt == 0), stop=(nt == nts[-1]))
            for h in range(4):
                PE(out=A_ps[:, h * 65:(h + 1) * 65], lhsT=ptm[:, h * 128:(h + 1) * 128], rhs=selmap[:, nt * 65:(nt + 1) * 65],
                   start=(nt == 0 and h == 0), stop=(nt == nts[-1]), skip_group_check=True)
        ACT(out=Osb[0][i % 2][:], in_=OcwB[:], func=AF.Identity, bias=tiny[:])
        V("tensor_copy", out=A_sb[:], in_=A_ps)
        A3 = A_sb[:].rearrange("p (h c) -> p h c", h=4)
        V("tensor_scalar", out=rdc[:], in0=A3[:, :, 64], scalar1=1e-30, scalar2=None, op0=ALU.max)
        V("reciprocal", out=rdc[:], in_=rdc[:])
        V("tensor_scalar", out=imp[:], in0=A3[:, 0, 0:64], scalar1=rdc[:, 0:1], scalar2=None, op0=ALU.mult)
        for h in range(1, 4):
            V("scalar_tensor_tensor", out=imp[:], in0=A3[:, h, 0:64], scalar=rdc[:, h:h + 1], in1=imp[:],
              op0=ALU.mult, op1=ALU.add)
        cc = 62 - 2 * i
        V("tensor_tensor", out=imp[:], in0=imp[:], in1=cmb[:, cc:cc + 64], op=ALU.mult)
        V("tensor_tensor", out=imp[:], in0=imp[:], in1=amb[:, cc:cc + 64], op=ALU.add)
        V("memset", ap=imp[:, 0:1], constant=3.0e4)
        V("max", out=m8[:, 0:8], in_=imp[:])
        V("match_replace", out=imp2[:], in_to_replace=m8[:, 0:8], in_values=imp[:], imm_value=-3.0e4)
        V("max", out=m8[:, 8:16], in_=imp2[:])
        V("tensor_scalar", out=nsel[:], in0=imp[:], scalar1=m8[:, 15:16], scalar2=NEG, op0=ALU.is_lt, op1=ALU.mult)

    def cmp_C(i):
        PE("transpose", out=nT_ps, in_=nsel[:], identity=identf[:])
        ACT(out=qn[64:128, i % 2, :].rearrange("p (h q) -> p h q", h=4), in_=nT_ps.unsqueeze(1).to_broadcast([64, 4, 128]),
            func=AF.Copy)

    def attn_loop(kT_sb, Vaug, kts, i, Obank, maskf):
        pts = []
        n = len(kts)
        pend = None
        for idx, kt in enumerate(kts):
            pt = score_tile(kT_sb, kt * 128, i, maskf(kt))
            if pend is not None:
                pidx, pkt, ppt = pend
                PE(out=Obank[:], lhsT=Vaug[:, pkt, :], rhs=ppt[:], start=(pidx == 0), stop=False)
            pend = (idx, kt, pt)
        pidx, pkt, ppt = pend
        PE(out=Obank[:], lhsT=Vaug[:, pkt, :], rhs=ppt[:], start=(pidx == 0), stop=True)

    def sel_mask(i):
        def f(kt):
            if kt == i:
                return (identb[:], negtri[:])
            return None
        return f

    def win_mask(i):
        def f(kt):
            if kt == i:
                return (identb[:], negtri[:])
            if kt == i - 4:
                return (identb[:], negtriw[:])
            return None
        return f

    def combine(i, osb_list):
        pg = 0
        c0 = i * 128
        for n_, (br, osb) in enumerate(osb_list):
            gbB = gbBs[br]
            for cb in range(4):
                h = cb
                hb = h * 3 + br
                lhsT = identb[pg:pg + 12, pg + hb:pg + hb + 1].to_broadcast([12, 64])
                PE(out=gbB[0:64, cb * 128:(cb + 1) * 128], lhsT=lhsT, rhs=gl[pg:pg + 12, c0:c0 + 128], start=True, stop=True)
        for n_, (br, osb) in enumerate(osb_list):
            gbB = gbBs[br]
            c_sb = c_sbs[br]
            tmp_sb = tmp_sbs[br]
            if i < 16:
                ACT(out=c_sb[:], in_=osb[64:128, :], func=AF.Ln)
                ACT(out=c_sb[:], in_=c_sb[:], func=AF.Exp, scale=-1.0)
            else:
                V("reciprocal", out=c_sb[:], in_=osb[64:128, :])
            V("tensor_tensor", out=c_sb[:], in0=gbB[0:64, :], in1=c_sb[:], op=ALU.mult)
            dst = y_sb if n_ == 0 else tmp_sb
            o4 = osb[0:64, :].rearrange("p (j r q) -> p j r q", j=2, r=2)
            c4 = c_sb[:].rearrange("p (j r q) -> p j r q", j=2, r=2)
            for hh in range(2):
                POOL("tensor_tensor", out=dst[64 * hh:64 * hh + 64, :].rearrange("p (j q) -> p j q", j=2), in0=o4[:, :, hh, :],
                     in1=c4[:, :, hh, :], op=ALU.mult)
            if n_ > 0:
                POOL("tensor_tensor", out=y_sb[:], in0=y_sb[:], in1=tmp_sb[:], op=ALU.add)
        V("tensor_tensor", out=YT[:, 1:3, i * 128:(i + 1) * 128], in0=y_sb[:].rearrange("p (j q) -> p j q", j=2),
          in1=ngs[:, :, i * 128:(i + 1) * 128], op=ALU.mult)

    NQ = DBG["nq"]
    if NQ > 0:
        cmp_A(0)
        cmp_B(0)
    for i in range(NQ):
        if i + 1 < NQ:
            cmp_A(i + 1)
        cmp_C(i)
        if i > 0:
            combine(i - 1, [(b_, Osb[b_][(i - 1) % 2]) for b_ in DBG.get("branches", (0, 1, 2))])
        kts = [kt for kt in range(i - 4, i + 1) if kt >= 0]
        attn_loop(kwT, Vw, kts, i, OcwB, win_mask(i))
        ACT(out=Osb[2][i % 2][:], in_=OcwB[:], func=AF.Identity, bias=tiny[:])
        if i + 1 < NQ:
            cmp_B(i + 1)
        attn_loop(ksT, Vs, list(range(i + 1)), i, OsB[i % 2], sel_mask(i))
        ACT(out=Osb[1][i % 2][:], in_=OsB[i % 2][:], func=AF.Identity, bias=tiny[:])
    if NQ > 0:
        combine(NQ - 1, [(b_, Osb[b_][(NQ - 1) % 2]) for b_ in DBG.get("branches", (0, 1, 2))])

    for c in range(4):
        mk.dma("sync", out=yt_d[c * 128:(c + 1) * 128, :], in_=YT[:, c, :], _force=True)
    mk.end_phase()
    es.close()
    pes.close()


def emit_B(nc, mk, banks, sfx, D, make_xT):
    yt_d = D["ytf"]; x_d = D["xr"]; wo_d = D["wo"]; g_d = D["lng"]; b_d = D["lnb"]; o_d = D["xo"]
    mk.sfx = sfx
    es = ExitStack()

    def T(name, shape, dt):
        return es.enter_context(nc.sbuf_tensor(name + sfx, list(shape), dt))

    wo = T("wo_sb", [128, 8, DM], BF16)
    gam = T("gam", [128, DM], F32)
    bet = T("bet", [128, DM], F32)
    ytb = T("ytb", [128, 2, 8, 512], BF16)
    xs = [T(f"xs{i}", [128, DM], F32) for i in range(3)]
    zs = [T(f"zs{i}", [128, DM], F32) for i in range(3)]
    ys = [T(f"ys{i}", [128, DM], F32) for i in range(2)]
    identb = T("identb_b", [128, 128], BF16)
    zb = T("zb", [128, DM], BF16)
    xTt = T("xTt", [128, 2, 8, 128], BF16)
    alpha = float((2.0 * 2) ** 0.25)

    mk.dma("gpsimd", out=wo[:, 0:4, :], in_=wo_d.rearrange("(c p) n -> p c n", p=128)[:, 0:4, :])
    mk.dma("gpsimd", out=wo[:, 4:8, :], in_=wo_d.rearrange("(c p) n -> p c n", p=128)[:, 4:8, :])
    mk.dma("gpsimd", out=identb[:], in_=D["ident"])
    mk.dma("sync", out=gam[:], in_=g_d.to_broadcast([128, DM]))
    mk.dma("sync", out=bet[:], in_=b_d.to_broadcast([128, DM]))
    yt_v = yt_d.rearrange("(c p) t -> p c t", p=128)
    xT_v = D["xTo"].rearrange("(c p) t -> p c t", p=128) if make_xT else None
    st6s = [T(f"bst6_{i}", [128, 2, 6], F32) for i in range(2)]
    mvs = [T(f"bmv_{i}", [128, 2], F32) for i in range(2)]
    rstds = [T(f"brstd_{i}", [128, 1], F32) for i in range(2)]
    NTB = S // 128

    def stage1(t):
        tb = t // 4
        if t % 4 == 0:
            for tbn in ([0, 1] if tb == 0 else [tb + 1]):
                if tbn < S // 512:
                    mk.dma("scalar", out=ytb[:, tbn % 2, 0:4, :], in_=yt_v[:, 0:4, tbn * 512:(tbn + 1) * 512], key="ytb%d" % (tbn % 2))
                    mk.dma("scalar", out=ytb[:, tbn % 2, 4:8, :], in_=yt_v[:, 4:8, tbn * 512:(tbn + 1) * 512], key="ytb%d" % (tbn % 2))
        x_sb = xs[t % 3]
        z = zs[t % 3]
        y_sb = ys[t % 2]
        mk.dma("scalar", out=x_sb[:], in_=x_d[t * 128:(t + 1) * 128, :])
        nbk = 6 if make_xT else 8
        for nh in range(2):
            ps = banks[(2 * t + nh) % nbk]
            for c in range(8):
                mk.op("tensor", "matmul", out=ps[:], lhsT=ytb[:, tb % 2, c, (t % 4) * 128:(t % 4 + 1) * 128],
                      rhs=wo[:, c, nh * 512:(nh + 1) * 512], start=(c == 0), stop=(c == 7))
            mk.op("scalar", "activation", out=y_sb[:, nh * 512:(nh + 1) * 512], in_=ps[:], func=AF.Copy)
            mk.op("vector", "scalar_tensor_tensor", out=z[:, nh * 512:(nh + 1) * 512], in0=x_sb[:, nh * 512:(nh + 1) * 512],
                  scalar=alpha, in1=y_sb[:, nh * 512:(nh + 1) * 512], op0=ALU.mult, op1=ALU.add)
            mk.op("vector", "bn_stats", out=st6s[t % 2][:, nh, :], in_=z[:, nh * 512:(nh + 1) * 512])
        mk.op("vector", "bn_aggr", out=mvs[t % 2][:], in_=st6s[t % 2][:].rearrange("p a b -> p (a b)"))
        mk.op("vector", "tensor_scalar", out=rstds[t % 2][:], in0=mvs[t % 2][:, 1:2], scalar1=1e-5, scalar2=None, op0=ALU.add)

    def stage2(t):
        z = zs[t % 3]
        mv = mvs[t % 2]
        rstd = rstds[t % 2]
        mk.op("scalar", "activation", out=rstd[:], in_=rstd[:], func=AF.Sqrt)
        mk.op("vector", "reciprocal", out=rstd[:], in_=rstd[:])
        mk.op("vector", "tensor_scalar", out=z[:], in0=z[:], scalar1=mv[:, 0:1], scalar2=rstd[:, 0:1], op0=ALU.subtract, op1=ALU.mult)
        mk.op("gpsimd", "tensor_tensor", out=z[:], in0=z[:], in1=gam[:], op=ALU.mult)
        mk.op("gpsimd", "tensor_tensor", out=z[:], in0=z[:], in1=bet[:], op=ALU.add)
        mk.dma("sync", out=o_d[t * 128:(t + 1) * 128, :], in_=z[:], key="outz%d" % (t % 3))
        if make_xT:
            mk.op("scalar", "activation", out=zb[:], in_=z[:], func=AF.Copy)
            bT = banks[6 + t % 2][:, :].bitcast(BF16)
            for c in range(8):
                mk.op("tensor", "transpose", out=bT[:, c * 128:(c + 1) * 128], in_=zb[:, c * 128:(c + 1) * 128], identity=identb[:])
            mk.op("scalar", "activation", out=xTt[:, t % 2].rearrange("p c q -> p (c q)"), in_=bT, func=AF.Copy)
            mk.dma("sync", out=xT_v[:, :, t * 128:(t + 1) * 128], in_=xTt[:, t % 2], key="xTo%d" % (t % 2))

    stage1(0)
    for t in range(NTB):
        if t + 1 < NTB:
            stage1(t + 1)
        stage2(t)
    mk.end_phase()
    es.close()


def build_F():
    nc = bass.Bass("TRN2", target_bir_lowering=False)

    def din(name, shape, dt=F32):
        return nc.dram_tensor(name, list(shape), dt, kind="ExternalInput").ap()

    xT0 = din("xT0", [DM, S]); x0 = din("x0", [S, DM])
    w_d = din("w", [4, DM, NCOL])
    cos_d = din("cos", [S, 32]); sin_d = din("sin", [S, 32])
    kd_d = din("kd", [2, 128, 2]); qd_d = din("qd", [2, 128, 128]); dt_d = din("dt", [2, 128, 256]); cdv_d = din("cdv", [2, 128, 1])
    mbig_d = din("mbig", [128, 2176]); negtri_d = din("negtri", [128, 512]); negtriw_d = din("negtriw", [128, 512])
    ident_d = din("ident", [128, 128]); selmap_d = din("selmap", [128, 130])
    cmb_d = din("cmb", [128, 126]); amb_d = din("amb", [128, 126]); esel_d = din("esel", [64, 4096])
    cmpw_d = din("cmpw", [2, 2, 2048, 64]); posT_d = din("posT", [2, 2, 64, 32])
    lruv_d = din("lruv", [4, 128, 9]); wa_d = din("wa", [4, 128, 128]); wx_d = din("wx", [4, 128, 128])
    wo_d = din("wo", [2, DM, DM]); g_d = din("lng", [2, 1, DM]); b_d = din("lnb", [2, 1, DM])
    xo = nc.dram_tensor("xo", [S, DM], F32, kind="ExternalOutput").ap()
    ytf = nc.dram_tensor("ytf_scr", [DM, S], BF16).ap()
    x1 = nc.dram_tensor("x1_scr", [S, DM], F32).ap()
    x1T = nc.dram_tensor("x1T_scr", [DM, S], BF16).ap()
    mk = MK(nc)
    banks = [nc.alloc_psum_tensor(f"bank{i}", [128, 512], F32) for i in range(8)]
    for l in range(DBG.get("nlayers", 2)):
        for hf in range(2):
            i4 = l * 2 + hf
            D = dict(xT=(xT0 if l == 0 else x1T), w=w_d[i4], cos=cos_d, sin=sin_d, kd=kd_d[hf], qd=qd_d[hf], dt=dt_d[hf],
                     cdv=cdv_d[hf], mbig=mbig_d, negtri=negtri_d, negtriw=negtriw_d, ident=ident_d, selmap=selmap_d,
                     cmb=cmb_d, amb=amb_d, esel=esel_d, cmpw=cmpw_d[l], posT=posT_d[l], lruv=lruv_d[i4], wa=wa_d[i4],
                     wx=wx_d[i4], yt=ytf[hf * 512:(hf + 1) * 512, :])
            emit_A(nc, mk, banks, "_a%d" % i4, D)
        last = (l == DBG.get("nlayers", 2) - 1)
        D = dict(ytf=ytf, xr=(x0 if l == 0 else x1), wo=wo_d[l], lng=g_d[l], lnb=b_d[l], xo=(xo if last else x1),
                 ident=ident_d, xTo=x1T)
        emit_B(nc, mk, banks, "_b%d" % l, D, make_xT=not last)
    return nc, mk


def f_inputs(inp, b):
    c0, c1 = make_consts(0), make_consts(1)
    m = {k: c0[k] for k in ("cos", "sin", "mbig", "negtri", "negtriw", "ident", "selmap", "cmb", "amb", "esel")}
    for k in ("kd", "qd", "dt", "cdv"):
        m[k] = np.ascontiguousarray(np.stack([c0[k], c1[k]]))
    xb = np.asarray(inp["x"][b], np.float32)
    m["x0"] = np.ascontiguousarray(xb)
    m["xT0"] = np.ascontiguousarray(xb.T)
    m["w"] = np.ascontiguousarray(np.stack([inp["w_in"][l][:, core_columns(hf)] for l in range(2) for hf in range(2)]))
    m["cmpw"] = np.ascontiguousarray(inp["nsa_cmp_w"])
    m["posT"] = np.ascontiguousarray(inp["nsa_cmp_pos"].transpose(0, 1, 3, 2))
    lruv = np.zeros((4, 128, 9), np.float32)
    wa = np.zeros((4, 128, 128), np.float32)
    wx = np.zeros((4, 128, 128), np.float32)
    for l in range(2):
        for hf in range(2):
            i4 = l * 2 + hf
            ch = slice(hf * 128, hf * 128 + 128)
            lruv[i4, :, 0:4] = inp["lru_conv_w"][l][:, ch].T
            lruv[i4, :, 4] = inp["lru_conv_b"][l][ch]
            lruv[i4, :, 5] = inp["lru_b_a"][l][ch]
            lruv[i4, :, 6] = inp["lru_b_x"][l][ch]
            lruv[i4, :, 7] = inp["lru_lambda"][l][ch]
            for j in range(2):
                wa[i4, j * 64:(j + 1) * 64, j * 64:(j + 1) * 64] = inp["lru_w_a"][l][2 * hf + j]
                wx[i4, j * 64:(j + 1) * 64, j * 64:(j + 1) * 64] = inp["lru_w_x"][l][2 * hf + j]
    m["lruv"] = lruv; m["wa"] = wa; m["wx"] = wx
    rows = np.concatenate([wout_rows(0), wout_rows(1)])
    m["wo"] = np.ascontiguousarray(np.stack([inp["w_out"][l][rows, :] for l in range(2)]))
    m["lng"] = np.ascontiguousarray(inp["ln_g"][:, None, :])
    m["lnb"] = np.ascontiguousarray(inp["ln_b"][:, None, :])
    return m


_PROG = {}


def kernel(**inputs):
    inp = {k: np.asarray(v) for k, v in inputs.items()}
    if "F" not in _PROG:
        _PROG["F"] = build_F()[0]
    in_maps = [f_inputs(inp, b) for b in range(4)]
    res = run_bass_kernel_spmd(_PROG["F"], in_maps, core_ids=list(range(4)))
    return np.stack([np.asarray(res.results[b]["xo"], dtype=np.float32) for b in range(4)])
```

```python
import math
from contextlib import ExitStack

import numpy as np
import ml_dtypes
import concourse.bass as bass
import concourse.mybir as mybir
from concourse.bass_utils import run_bass_kernel_spmd

F32 = mybir.dt.float32
BF16 = mybir.dt.bfloat16
ALU = mybir.AluOpType
AF = mybir.ActivationFunctionType

S = 4096
DM = 1024
NEG = -30000.0
DBG = {"nblocks": 8, "ret": True, "lru": True, "cmp": True, "nq": 32}

_AP_KW = ("in_", "in0", "in1", "lhsT", "rhs", "scalar", "scalar1", "scalar2", "bias", "scale",
          "data0", "data1", "initial", "in_to_replace", "in_values", "identity")
_ESZ = {F32: 4, BF16: 2}


def _is_ap(v):
    return hasattr(v, "tensor") and hasattr(v, "ap") and hasattr(v, "offset")


def _region(ap):
    t = ap.tensor
    name = t.name
    pat = list(ap.ap)
    esz = _ESZ.get(ap.dtype, 4)
    if "DRam" in type(t).__name__:
        lo = ap.offset
        hi = lo + sum((c - 1) * abs(s) for s, c in pat) + 1
        return (name, 0, 1, lo * esz, hi * esz, False)
    pstep, pcnt = pat[0]
    p0 = ap.start_partition()
    if pstep == 0:
        foff = ap.offset
    else:
        foff = ap.offset - p0 * pstep
    lo = foff
    hi = lo + sum((c - 1) * abs(s) for s, c in pat[1:]) + 1
    return (name, p0, p0 + pcnt, lo * esz, hi * esz, "PSum" in type(t).__name__)


class MK:
    ENG = ("tensor", "vector", "scalar", "gpsimd", "sync")

    def __init__(self, nc):
        self.nc = nc
        self.esem = {e: nc.alloc_semaphore("es_" + e) for e in self.ENG}
        self.nidx = {e: 0 for e in self.ENG}
        self.base = {e: 0 for e in self.ENG}
        self.dsem = {}
        self.dcount = {}
        self.n_inst = 0
        self.sfx = ""
        self.log = []
        self._reset()

    def _reset(self):
        self.streams = {e: [] for e in self.ENG}
        self.needed = {e: set() for e in self.ENG}
        self.known = {e: {} for e in self.ENG}
        self.acc = {}

    def _deps_and_record(self, tok, reads, writes):
        deps = {}
        rr = [_region(ap) for ap in reads]
        wr = [_region(ap) for ap in writes]
        for (name, p0, p1, lo, hi, ps) in rr:
            for r in self.acc.setdefault(name, []):
                if (ps and r[4] != tok[0]) or (r[6] and r[0] < p1 and p0 < r[1] and r[2] < hi and lo < r[3]):
                    if deps.get(r[4], -1) < r[5]:
                        deps[r[4]] = r[5]
        for (name, p0, p1, lo, hi, ps) in wr:
            for r in self.acc.setdefault(name, []):
                if (ps and r[4] != tok[0]) or (r[0] < p1 and p0 < r[1] and r[2] < hi and lo < r[3]):
                    if deps.get(r[4], -1) < r[5]:
                        deps[r[4]] = r[5]
        for (name, p0, p1, lo, hi, ps) in rr + wr:
            if ps:
                recs = self.acc[name]
                recs[:] = [r for r in recs if r[4] == tok[0]]
        for (name, p0, p1, lo, hi, ps) in rr:
            recs = self.acc[name]
            recs[:] = [r for r in recs if not ((not r[6]) and r[4] == tok[0] and p0 <= r[0] and r[1] <= p1
                                               and lo <= r[2] and r[3] <= hi)]
            recs.append((p0, p1, lo, hi, tok[0], tok[1], False))
        for (name, p0, p1, lo, hi, ps) in wr:
            recs = self.acc[name]
            recs[:] = [r for r in recs if not (p0 <= r[0] and r[1] <= p1 and lo <= r[2] and r[3] <= hi)]
            recs.append((p0, p1, lo, hi, tok[0], tok[1], True))
        return deps

    def _emit_waits(self, eng, deps):
        kn = self.known[eng]
        self._lw = []
        for k, v in deps.items():
            if k.startswith("d:"):
                v = self.dcount[k]
            if eng == "tensor" and k == "e:tensor":
                continue
            if kn.get(k, -1) >= v:
                continue
            kn[k] = v
            self._lw.append(k)
            if k.startswith("e:"):
                self.needed[k[2:]].add(v)
            self.streams[eng].append(("wait", k, v))

    def op(self, eng, method, **kw):
        if DBG.get("limit") is not None and self.n_inst >= DBG["limit"]:
            return
        writes = [kw["out"]] if "out" in kw else []
        if kw.get("accum_out", None) is not None:
            writes.append(kw["accum_out"])
        if method == "memset":
            writes.append(kw["ap"])
        reads = [kw[k] for k in _AP_KW if k in kw and _is_ap(kw[k])]
        reads += list(kw.pop("_reads", ()))
        writes += list(kw.pop("_writes", ()))
        idx = self.nidx[eng]
        self.nidx[eng] += 1
        deps = self._deps_and_record(("e:" + eng, idx), reads, writes)
        self._emit_waits(eng, deps)
        self.streams[eng].append(("op", method, kw, idx))
        if DBG.get("log"):
            o = kw.get("out", kw.get("ap"))
            self.log.append((eng, method, o.tensor.name if o is not None else "", tuple(self._lw)))
        self.n_inst += 1

    def dma(self, eng, out, in_, key=None, **kw):
        if DBG.get("limit") is not None and self.n_inst >= DBG["limit"] and not kw.pop("_force", False):
            return
        kw.pop("_force", None)
        if key is None:
            key = out.tensor.name if "DRam" not in type(out.tensor).__name__ else in_.tensor.name
            if self.sfx and key.endswith(self.sfx):
                key = key[:-len(self.sfx)]
        key = "d:" + key
        if key not in self.dsem:
            self.dsem[key] = self.nc.alloc_semaphore("ds_" + key[2:])
            self.dcount[key] = 0
        val = self.dcount[key] + 16
        deps = self._deps_and_record((key, val), [in_], [out])
        self._emit_waits(eng, deps)
        self.dcount[key] = val
        self.streams[eng].append(("dma", out, in_, key, kw))
        self.n_inst += 1

    def coll(self, kind, rg, in_, out, key="cc"):
        eng = "gpsimd"
        key = "d:" + key
        if key not in self.dsem:
            self.dsem[key] = self.nc.alloc_semaphore("ds_" + key[2:])
            self.dcount[key] = 0
        val = self.dcount[key] + 16
        deps = self._deps_and_record((key, val), [in_], [out])
        self._emit_waits(eng, deps)
        self.dcount[key] = val
        self.streams[eng].append(("coll", kind, rg, in_, out, key))
        self.n_inst += 1

    def end_phase(self):
        for k, v in self.dcount.items():
            if self.known["sync"].get(k, -1) < v:
                self.streams["sync"].append(("wait", k, v))
        valmap = {}
        for e in self.ENG:
            s = sorted(self.needed[e])
            valmap[e] = {idx: self.base[e] + i + 1 for i, idx in enumerate(s)}

        def run_stream(e, engobj):
            vm = valmap[e]
            for item in self.streams[e]:
                if item[0] == "wait":
                    _, k, v = item
                    if k.startswith("e:"):
                        engobj.wait_ge(self.esem[k[2:]], valmap[k[2:]][v])
                    else:
                        engobj.wait_ge(self.dsem[k], v)
                elif item[0] == "op":
                    _, method, kw, idx = item
                    inst = getattr(engobj, method)(**kw)
                    if idx in vm:
                        inst.then_inc(self.esem[e], 1)
                elif item[0] == "coll":
                    _, kind, rg, in_, out, key = item
                    engobj.collective_compute(kind, ALU.bypass, replica_groups=rg, ins=[in_], outs=[out]).then_inc(self.dsem[key], 16)
                else:
                    _, out, in_, key, kw = item
                    engobj.dma_start(out=out, in_=in_, **kw).then_inc(self.dsem[key], 16)

        with self.nc.Block() as blk:
            for e in self.ENG:
                if not self.streams[e]:
                    continue
                getattr(blk, e)(lambda engobj, e=e: run_stream(e, engobj))
        for e in self.ENG:
            self.base[e] += len(self.needed[e])
        self._reset()


OFF = dict(rq=0, rk=256, rv=512, rg=768, nq=1024, nkc=1536, nvc=1664, nks=1792, nvs=1920, nkw=2048,
           nvw=2176, ng=2304, ngl=2816, lx=2840, lg=3096)
NFM = 8
GLC = NFM * 128
TM0 = GLC + 12
NCOL = TM0 + 640


def core_columns(hf):
    r = np.arange
    g = hf
    cols = []
    cols += list(OFF["nq"] + g * 256 + r(128))
    cols += list(OFF["nq"] + g * 256 + 128 + r(128))
    cols += list(OFF["nks"] + g * 64 + r(64)) + list(OFF["nkw"] + g * 64 + r(64))
    cols += list(OFF["nkc"] + g * 64 + r(64)) + list(OFF["nvc"] + g * 64 + r(64))
    cols += list(OFF["ng"] + g * 256 + r(128))
    cols += list(OFF["ng"] + g * 256 + 128 + r(128))
    cols += list(OFF["lx"] + hf * 128 + r(128))
    cols += list(OFF["lg"] + hf * 128 + r(128))
    cols += list(OFF["ngl"] + g * 12 + r(12))
    cols += list(OFF["nvs"] + g * 64 + r(64)) + list(OFF["nvw"] + g * 64 + r(64))
    for nm in ("rq", "rk", "rv", "rg"):
        cols += list(OFF[nm] + hf * 128 + r(128))
    assert len(cols) == NCOL
    return np.array(cols)


def wout_rows(hf):
    r = np.arange
    return np.concatenate([hf * 128 + r(128), 256 + hf * 256 + r(256), 768 + hf * 128 + r(128)])


_CONST_CACHE = {}


def make_consts(hf):
    if hf in _CONST_CACHE:
        return _CONST_CACHE[hf]
    c = {}
    half = 32
    inv = (1.0 / (10000.0 ** (np.arange(half, dtype=np.float32) / half))).astype(np.float32)
    ang = np.arange(S, dtype=np.float32)[:, None] * inv[None, :]
    c["cos"] = np.cos(ang).astype(np.float32)
    c["sin"] = np.sin(ang).astype(np.float32)
    hs = np.array([2 * hf, 2 * hf + 1], dtype=np.float32)
    log_g = np.log(1.0 - 2.0 ** (-5.0 - hs)).astype(np.float32)
    idx = np.arange(128, dtype=np.float32)
    kd = np.exp((127.0 - idx)[:, None] * log_g[None, :]) / 8.0
    c["kd"] = kd.astype(np.float32)
    qd = np.exp((idx + 1.0)[None, :] * log_g[:, None])
    c["qd"] = np.repeat(qd, 64, axis=0).astype(np.float32)
    diff = idx[None, :] - idx[:, None]
    dt = np.where(diff[None] >= 0, np.exp(np.maximum(diff, 0.0)[None] * log_g[:, None, None]), 0.0) / 8.0
    c["dt"] = np.ascontiguousarray(dt.transpose(1, 0, 2)).reshape(128, 256).astype(np.float32)
    c["cdv"] = np.repeat(np.exp(128.0 * log_g), 64)[:, None].astype(np.float32)
    n = np.arange(128)[:, None]
    cc = np.arange(2176)[None, :]
    c["mbig"] = (cc >= 16 * n + 31).astype(np.float32)
    k = np.arange(128)[:, None]
    q = np.arange(128)[None, :]
    c["negtri"] = np.tile(np.where(k > q, NEG, 0.0), (1, 4)).astype(np.float32)
    c["negtriw"] = np.tile(np.where(k <= q, NEG, 0.0), (1, 4)).astype(np.float32)
    c["ident"] = np.eye(128, dtype=np.float32)
    ci = np.arange(256)[:, None]
    sj = np.arange(64)[None, :]
    ov = np.minimum(ci * 16 + 32, sj * 64 + 64) - np.maximum(ci * 16, sj * 64)
    sm = np.clip(ov, 0, None) / 16.0
    sm[255] = 0.0
    sma = np.concatenate([sm, np.ones((256, 1))], axis=1)
    sma[255] = 0.0
    c["selmap"] = np.ascontiguousarray(sma.reshape(2, 128, 65).transpose(1, 0, 2)).reshape(128, 130).astype(np.float32)
    rel = np.arange(126)[None, :] - 62
    relcur = (np.arange(128)[:, None] >= 64).astype(np.int64)
    cm = (rel < relcur - 1).astype(np.float32)
    am = np.where(rel == relcur, 2.0e4, np.where(rel == relcur - 1, 1.0e4, np.where(rel > relcur, -1.0e4, 0.0)))
    jj = np.arange(64)[:, None]
    kk = np.arange(4096)[None, :]
    c["esel"] = (jj == kk // 64).astype(np.float32)
    c["cmb"] = cm.astype(np.float32)
    c["amb"] = am.astype(np.float32)
    _CONST_CACHE[hf] = c
    return c


def emit_A(nc, mk, banks, sfx, D):
    xT_d = D["xT"]; w_d = D["w"]
    cos_d = D["cos"]; sin_d = D["sin"]
    kd_d = D["kd"]; qd_d = D["qd"]; dt_d = D["dt"]
    mbig_d = D["mbig"]; negtri_d = D["negtri"]; negtriw_d = D["negtriw"]
    ident_d = D["ident"]; selmap_d = D["selmap"]
    cmb_d = D["cmb"]; amb_d = D["amb"]
    cmpw_d = D["cmpw"]; posT_d = D["posT"]
    lruv_d = D["lruv"]
    wa_d = D["wa"]; wx_d = D["wx"]
    cdv_d = D["cdv"]
    esel_d = D["esel"]
    yt_d = D["yt"]
    mk.sfx = sfx
    pes = ExitStack()

    def A(name, shape, dt):
        return pes.enter_context(nc.sbuf_tensor(name + sfx, list(shape), dt))

    qT = A("qT", [128, 32, 2, 128], BF16)
    ksT = A("ksT", [128, S], BF16)
    kwT = A("kwT", [128, S], BF16)
    kvcT = A("kvcT", [128, S], BF16)
    Vs = A("Vs", [128, 32, 128], BF16)
    Vw = A("Vw", [128, 32, 128], BF16)
    ngs = A("ngs", [128, 2, S], BF16)
    gl = A("gl", [32, S], BF16)
    YT = A("YT", [128, 4, S], BF16)
    mbig = A("mbig_sb", [128, 2176], BF16)
    identb = A("identb", [128, 128], BF16)
    identf = A("identf", [128, 128], F32)
    negtri = A("negtri_sb", [128, 512], BF16)
    negtriw = A("negtriw_sb", [128, 512], BF16)
    selmap = A("selmap_sb", [128, 130], BF16)
    cmb = A("cmb_sb", [128, 126], F32)
    amb = A("amb_sb", [128, 126], F32)
    kcT = A("kcT", [128, 256], BF16)
    Vc = A("Vc", [128, 2, 128], BF16)
    tiny = A("tiny", [128, 1], F32)

    def V(method, **kw):
        mk.op("vector", method, **kw)

    def ACT(method="activation", **kw):
        mk.op("scalar", method, **kw)

    def PE(method="matmul", **kw):
        mk.op("tensor", method, **kw)

    def POOL(method, **kw):
        mk.op("gpsimd", method, **kw)

    es = ExitStack()

    def T(name, shape, dt):
        return es.enter_context(nc.sbuf_tensor(name + sfx, list(shape), dt))

    w_sb = T("w_sb", [128, 8, NCOL], BF16)
    xt = T("xt", [128, 2, 8, 512], BF16)
    cos_sb = T("cos_sb", [128, 32, 32], F32)
    sin_sb = T("sin_sb", [128, 32, 32], F32)
    kd_sb = T("kd_sb", [128, 2], F32)
    qd_sb = T("qd_sb", [128, 128], F32)
    dt_sb = T("dt_sb", [128, 256], F32)
    Wk = T("Wk", [128, 32, 64], BF16)
    posT = T("posT_sb", [128, 32], BF16)
    lruv = T("lruv_sb", [128, 9], F32)
    cdv = T("cdv_sb", [128, 1], F32)
    wa_sb = T("wa_sb", [128, 128], BF16)
    wx_sb = T("wx_sb", [128, 128], BF16)
    qk_sb = T("qk_sb", [128, 2, 256], F32)
    rt = [T(f"rt{i}", [128, 128], F32) for i in range(2)]
    rot2 = T("rot", [128, 2, 256], BF16)
    v_sb = T("v_sb", [128, 2, 128], BF16)
    gs_sb = T("gs_sb", [128, 2, 128], F32)
    kdec2 = T("kdec", [128, 2, 128], BF16)
    qrT = T("qrT", [128, 256], BF16)
    qdT = T("qdT", [128, 128], BF16)
    krT = T("krT", [128, 128], BF16)
    scT_sb = T("scT_sb", [128, 256], BF16)
    state_f = T("state_f", [128, 128], F32)
    state_bf = T("state_bf", [128, 2, 128], BF16)
    st6 = T("st6", [128, 2, 6], F32)
    mv = T("mv", [128, 2, 2], F32)
    rstd = T("rstd", [128, 2], F32)
    yn = T("yn", [128, 128], F32)
    yg = T("yg", [128, 128], BF16)
    lxb = T("lxb", [128, 2, 515], F32)
    xc = T("xc", [128, 512], F32)
    xcb = T("xcb", [128, 512], BF16)
    r_sb = T("r_sb", [128, 512], F32)
    i_sb = T("i_sb", [128, 512], F32)
    a_sb = T("a_sb", [128, 512], F32)
    u_sb = T("u_sb", [128, 512], F32)
    h_sb = T("h_sb", [128, 512], F32)
    hlast = T("hlast", [128, 1], F32)
    lgs = T("lgs", [128, 512], BF16)
    sp = T("sp", [128, 4], F32)
    biask = T("biask", [128, 1], F32)
    biasv = T("biasv", [128, 64], BF16)
    one_f = T("one_f", [128, 1], F32)

    for c in range(8):
        mk.dma("gpsimd", out=w_sb[:, c, :], in_=w_d[c * 128:(c + 1) * 128, :])
    mk.dma("sync", out=cos_sb[:], in_=cos_d.rearrange("(c p) f -> p c f", p=128))
    mk.dma("sync", out=sin_sb[:], in_=sin_d.rearrange("(c p) f -> p c f", p=128))
    mk.dma("sync", out=kd_sb[:], in_=kd_d)
    mk.dma("sync", out=qd_sb[:], in_=qd_d)
    mk.dma("sync", out=dt_sb[:], in_=dt_d)
    mk.dma("sync", out=identf[:], in_=ident_d)
    mk.dma("sync", out=cmb[:], in_=cmb_d)
    mk.dma("sync", out=amb[:], in_=amb_d)
    mk.dma("sync", out=lruv[:], in_=lruv_d)
    mk.dma("sync", out=cdv[:], in_=cdv_d)
    mk.dma("gpsimd", out=mbig[:, 0:1088], in_=mbig_d[:, 0:1088])
    mk.dma("gpsimd", out=mbig[:, 1088:2176], in_=mbig_d[:, 1088:2176])
    mk.dma("gpsimd", out=identb[:], in_=ident_d)
    mk.dma("gpsimd", out=negtri[:], in_=negtri_d)
    mk.dma("gpsimd", out=negtriw[:], in_=negtriw_d)
    mk.dma("gpsimd", out=selmap[:], in_=selmap_d)
    mk.dma("gpsimd", out=wa_sb[:], in_=wa_d)
    mk.dma("gpsimd", out=wx_sb[:], in_=wx_d)
    mk.dma("gpsimd", out=Wk[0:64, :, :], in_=cmpw_d[0].rearrange("(l d) e -> d l e", d=64))
    mk.dma("gpsimd", out=Wk[64:128, :, :], in_=cmpw_d[1].rearrange("(l d) e -> d l e", d=64))
    mk.dma("gpsimd", out=posT[0:64, :], in_=posT_d[0])
    mk.dma("gpsimd", out=posT[64:128, :], in_=posT_d[1])
    V("memset", ap=tiny[0:64, :], constant=0.0)
    V("memset", ap=tiny[64:128, :], constant=1e-18)
    V("memset", ap=qrT[:], constant=0.0)
    V("memset", ap=state_f[:], constant=0.0)
    V("memset", ap=one_f[:], constant=1.0)
    V("memset", ap=Vs[:, :, 64:128], constant=1.0)
    V("memset", ap=Vw[:, :, 64:128], constant=1.0)
    V("memset", ap=Vc[:], constant=0.0)
    V("memset", ap=kcT[:], constant=0.0)
    V("memset", ap=kwT[64:128, :], constant=0.0)
    mk.dma("gpsimd", out=ksT[64:128, 0:2048], in_=esel_d[:, 0:2048], key="esel")
    mk.dma("gpsimd", out=ksT[64:128, 2048:4096], in_=esel_d[:, 2048:4096], key="esel")
    V("memset", ap=lxb[:, 1, 512:515], constant=0.0)
    ACT(out=sp[:, 0:1], in_=lruv[:, 7:8], func=AF.Exp, scale=-1.0)
    V("tensor_scalar", out=sp[:, 0:1], in0=sp[:, 0:1], scalar1=1.0, scalar2=None, op0=ALU.add)
    ACT(out=sp[:, 1:2], in_=sp[:, 0:1], func=AF.Ln)
    V("tensor_scalar", out=sp[:, 2:3], in0=sp[:, 1:2], scalar1=-8.0, scalar2=None, op0=ALU.mult)
    V("tensor_scalar", out=sp[:, 3:4], in0=sp[:, 1:2], scalar1=-16.0, scalar2=None, op0=ALU.mult)

    xT_v = xT_d.rearrange("(c p) t -> p c t", p=128)
    fm_bank = [banks[0], banks[1]]
    fmi = 0
    psA = banks[2]
    psB = banks[3]
    bankT = banks[4]
    tq_ps = bankT[:, 0:64].bitcast(BF16)
    tk_ps = bankT[:, 64:128].bitcast(BF16)
    ty_ps = banks[7][:, 0:64].bitcast(BF16)
    scT_ps = banks[5][:, 0:256]
    y_ps = banks[5][:, 256:384]
    kv_ps = banks[5][:, 384:512]
    ps_r = banks[6]
    ps_i = banks[7]

    NB = DBG["nblocks"]

    def xload(tbn):
        xbn = xt[:, tbn % 2]
        mk.dma("gpsimd", out=xbn[:, 0:4, :], in_=xT_v[:, 0:4, tbn * 512:tbn * 512 + 512], key="xt%d" % (tbn % 2))
        mk.dma("gpsimd", out=xbn[:, 4:8, :], in_=xT_v[:, 4:8, tbn * 512:tbn * 512 + 512], key="xt%d" % (tbn % 2))

    fmc = [0]

    def G(tb, g):
        xb = xt[:, tb % 2]
        t0 = tb * 512
        M = 128 if g < NFM else 12
        ps = fm_bank[fmc[0] % 2]
        fmc[0] += 1
        for c in range(8):
            PE(out=ps[0:M, :], lhsT=w_sb[:, c, g * 128:g * 128 + M], rhs=xb[:, c, :], start=(c == 0), stop=(c == 7))
        if g in (0, 1):
            V("tensor_copy", out=qT[:, tb * 4:(tb + 1) * 4, g, :], in_=ps[:].rearrange("p (a q) -> p a q", a=4))
        elif g == 2:
            ACT(out=ksT[0:64, t0:t0 + 512], in_=ps[0:64, :], func=AF.Copy)
            V("tensor_copy", out=kwT[0:64, t0:t0 + 512], in_=ps[64:128, :])
        elif g == 3:
            ACT(out=kvcT[:, t0:t0 + 512], in_=ps[:], func=AF.Copy)
        elif g in (4, 5):
            ACT(out=ngs[:, g - 4, t0:t0 + 512], in_=ps[:], func=AF.Silu)
        elif g == 6:
            V("tensor_copy", out=lxb[:, tb % 2, 3:515], in_=ps[:])
        elif g == 7:
            ACT(out=lgs[:], in_=ps[:], func=AF.Silu)
        else:
            ACT(out=gl[0:12, t0:t0 + 512], in_=ps[0:12, :], func=AF.Sigmoid)

    def TM(ch):
        tb, tt = ch // 4, ch % 4
        xb = xt[:, tb % 2]
        for c in range(8):
            PE(out=psB[:, 0:128], lhsT=xb[:, c, tt * 128:(tt + 1) * 128], rhs=w_sb[:, c, TM0:TM0 + 128],
               start=(c == 0), stop=(c == 7))
        for c in range(8):
            PE(out=psA[:], lhsT=xb[:, c, tt * 128:(tt + 1) * 128], rhs=w_sb[:, c, TM0 + 128:TM0 + 640],
               start=(c == 0), stop=(c == 7))
        V("tensor_copy", out=Vs[:, ch, 0:64], in_=psB[:, 0:64])
        V("tensor_copy", out=Vw[:, ch, 0:64], in_=psB[:, 64:128])
        ACT(out=qk_sb[:, ch % 2, :], in_=psA[:, 0:256], func=AF.Copy)
        ACT(out=v_sb[:, ch % 2, :], in_=psA[:, 256:384], func=AF.Copy)
        ACT(out=gs_sb[:, ch % 2, :], in_=psA[:, 384:512], func=AF.Silu)

    def R0(ch):
        qs4 = qk_sb[:, ch % 2, :].rearrange("p (a b c) -> p a b c", a=4, b=2)
        rot = rot2[:, ch % 2, :]
        kdec = kdec2[:, ch % 2, :]
        rot4 = rot.rearrange("p (a b c) -> p a b c", a=4, b=2)
        cosb = cos_sb[:, ch, :].unsqueeze(1).to_broadcast([128, 4, 32])
        sinb = sin_sb[:, ch, :].unsqueeze(1).to_broadcast([128, 4, 32])
        r0 = rt[0][:].rearrange("p (a c) -> p a c", a=4)
        r1 = rt[1][:].rearrange("p (a c) -> p a c", a=4)
        V("tensor_tensor", out=r0, in0=qs4[:, :, 0, :], in1=cosb, op=ALU.mult)
        V("tensor_tensor", out=r1, in0=qs4[:, :, 1, :], in1=sinb, op=ALU.mult)
        V("tensor_tensor", out=rot4[:, :, 0, :], in0=r0, in1=r1, op=ALU.subtract)
        V("tensor_tensor", out=r0, in0=qs4[:, :, 0, :], in1=sinb, op=ALU.mult)
        V("tensor_tensor", out=r1, in0=qs4[:, :, 1, :], in1=cosb, op=ALU.mult)
        V("tensor_tensor", out=rot4[:, :, 1, :], in0=r0, in1=r1, op=ALU.add)
        POOL("tensor_tensor", out=kdec.rearrange("p (h d) -> p h d", h=2),
             in0=rot[:, 128:256].rearrange("p (h d) -> p h d", h=2),
             in1=kd_sb[:].unsqueeze(2).to_broadcast([128, 2, 64]), op=ALU.mult)

    def R1(ch):
        PE("transpose", out=tq_ps, in_=rot2[:, ch % 2, 0:128], identity=identb[:])
        PE("transpose", out=tk_ps, in_=rot2[:, ch % 2, 128:256], identity=identb[:])
        ACT(out=qrT[0:64, 0:128], in_=tq_ps[0:64, :], func=AF.Copy)
        ACT(out=qrT[64:128, 128:256], in_=tq_ps[64:128, :], func=AF.Copy)
        V("tensor_tensor", out=qdT[:], in0=tq_ps, in1=qd_sb[:], op=ALU.mult)
        ACT(out=krT[:], in_=tk_ps, func=AF.Copy)

    def R2(ch):
        PE(out=scT_ps, lhsT=krT[:], rhs=qrT[:], start=True, stop=True)
        V("tensor_tensor", out=scT_sb[:], in0=scT_ps, in1=dt_sb[:], op=ALU.mult)

    def R3(ch):
        vv = v_sb[:, ch % 2, :]
        for h in range(2):
            PE(out=y_ps[:, h * 64:(h + 1) * 64], lhsT=scT_sb[:, h * 128:(h + 1) * 128], rhs=vv[:, h * 64:(h + 1) * 64],
               start=(h == 0), stop=(ch == 0 and h == 1), skip_group_check=True)
        if ch > 0:
            PE(out=y_ps, lhsT=qdT[:], rhs=state_bf[:, ch % 2, :], start=False, stop=True, skip_group_check=True)
        PE(out=kv_ps, lhsT=kdec2[:, ch % 2, :], rhs=vv, start=True, stop=True)
        for h in range(2):
            V("bn_stats", out=st6[:, h, :], in_=y_ps[:, h * 64:(h + 1) * 64])
            V("bn_aggr", out=mv[:, h, :], in_=st6[:, h, :])
        V("tensor_scalar", out=rstd[:], in0=mv[:, :, 1], scalar1=1e-5, scalar2=None, op0=ALU.add)
        ACT(out=rstd[:], in_=rstd[:], func=AF.Sqrt)
        V("reciprocal", out=rstd[:], in_=rstd[:])
        for h in range(2):
            V("tensor_scalar", out=yn[:, h * 64:(h + 1) * 64], in0=y_ps[:, h * 64:(h + 1) * 64],
              scalar1=mv[:, h, 0:1], scalar2=rstd[:, h:h + 1], op0=ALU.subtract, op1=ALU.mult)
        for h in range(2):
            hp = slice(64 * h, 64 * h + 64)
            hc = slice(64 * h, 64 * h + 64)
            if ch == 0:
                V("tensor_copy", out=state_f[hp, hc], in_=kv_ps[hp, 64 * h:64 * h + 64])
            else:
                V("scalar_tensor_tensor", out=state_f[hp, hc], in0=state_f[hp, hc], scalar=cdv[hp, 0:1],
                  in1=kv_ps[hp, 64 * h:64 * h + 64], op0=ALU.mult, op1=ALU.add)
        POOL("tensor_copy", out=state_bf[:, (ch + 1) % 2, :], in_=state_f[:])
        V("tensor_tensor", out=yg[:], in0=yn[:], in1=gs_sb[:, ch % 2, :], op=ALU.mult)

    def R4(ch):
        PE("transpose", out=ty_ps, in_=yg[:], identity=identb[:])
        ACT(out=YT[:, 0, ch * 128:(ch + 1) * 128], in_=ty_ps, func=AF.Copy)

    def LRU_a(tb):
        lx = lxb[:, tb % 2]
        if tb == 0:
            V("memset", ap=lx[:, 0:3], constant=0.0)
        else:
            V("tensor_copy", out=lx[:, 0:3], in_=lxb[:, (tb - 1) % 2, 512:515])
        V("tensor_scalar", out=xc[:], in0=lx[:, 0:512], scalar1=lruv[:, 0:1], scalar2=lruv[:, 4:5], op0=ALU.mult, op1=ALU.add)
        for w in range(1, 4):
            V("scalar_tensor_tensor", out=xc[:], in0=lx[:, w:w + 512], scalar=lruv[:, w:w + 1], in1=xc[:],
              op0=ALU.mult, op1=ALU.add)
        V("tensor_copy", out=xcb[:], in_=xc[:])

    def LRU_b(tb):
        t0 = tb * 512
        PE(out=ps_r[:], lhsT=wa_sb[:], rhs=xcb[:], start=True, stop=True)
        PE(out=ps_i[:], lhsT=wx_sb[:], rhs=xcb[:], start=True, stop=True)
        ACT(out=r_sb[:], in_=ps_r[:], func=AF.Sigmoid, bias=lruv[:, 5:6])
        ACT(out=i_sb[:], in_=ps_i[:], func=AF.Sigmoid, bias=lruv[:, 6:7])
        ACT(out=a_sb[:], in_=r_sb[:], func=AF.Exp, scale=sp[:, 2:3])
        ACT(out=r_sb[:], in_=r_sb[:], func=AF.Exp, scale=sp[:, 3:4])
        ACT(out=r_sb[:], in_=r_sb[:], func=AF.Sqrt, scale=-1.0, bias=one_f[:])
        POOL("tensor_tensor", out=u_sb[:], in0=i_sb[:], in1=xc[:], op=ALU.mult)
        V("tensor_tensor", out=u_sb[:], in0=u_sb[:], in1=r_sb[:], op=ALU.mult)
        init = 0.0 if tb == 0 else hlast[:, 0:1]
        V("tensor_tensor_scan", out=h_sb[:], data0=a_sb[:], data1=u_sb[:], initial=init, op0=ALU.mult, op1=ALU.add)
        V("tensor_copy", out=hlast[:], in_=h_sb[:, 511:512])
        V("tensor_tensor", out=YT[:, 3, t0:t0 + 512], in0=h_sb[:], in1=lgs[:], op=ALU.mult)

    FILL = [[0, 1, 2], [3, 4], [5, 6], [7, 8]]
    NCH = NB * 4
    for ch in range(NCH):
        tb, tt = ch // 4, ch % 4
        if tt == 0:
            for tbn in ([0, 1] if tb == 0 else [tb + 1]):
                if tbn < NB:
                    xload(tbn)
        fl = list(FILL[tt])
        TM(ch)
        if DBG["ret"]:
            R0(ch)
        if ch > 1 and DBG["ret"]:
            R4(ch - 2)
        if ch > 0 and DBG["ret"]:
            R1(ch - 1)
        G(tb, fl.pop(0))
        if ch > 0 and DBG["ret"]:
            R2(ch - 1)
        G(tb, fl.pop(0))
        if ch > 0 and DBG["ret"]:
            R3(ch - 1)
        if fl:
            G(tb, fl.pop(0))
        if tt == 3 and DBG["lru"]:
            LRU_a(tb)
        if tt == 0 and tb > 0 and DBG["lru"]:
            LRU_b(tb - 1)
    if DBG["lru"] and NB > 0:
        LRU_b(NB - 1)
    if DBG["ret"] and NCH > 0:
        if NCH > 1:
            R4(NCH - 2)
        R1(NCH - 1); R2(NCH - 1); R3(NCH - 1); R4(NCH - 1)

    cps = banks[0]
    for l in range(32 if DBG["cmp"] else 0):
        PE(out=cps[0:64, 0:255], lhsT=Wk[0:64, l, :],
           rhs=kvcT[0:64, l:l + 16 * 254 + 1:16], start=(l == 0), stop=(l == 31))
    bps = banks[1]
    for l in range(32 if DBG["cmp"] else 0):
        PE(out=bps[0:64, 0:1], lhsT=Wk[0:64, l, :],
           rhs=posT[0:64, l:l + 1], start=(l == 0), stop=(l == 31))
    if DBG["cmp"]:
        V("tensor_copy", out=biask[0:64, :], in_=bps[0:64, 0:1])
        ACT(out=kcT[0:64, 0:255], in_=cps[0:64, 0:255], func=AF.Identity, bias=biask[0:64, :])
    bvp = banks[2]
    for l in range(32 if DBG["cmp"] else 0):
        PE(out=bvp[0:128, 0:64], lhsT=posT[64:128, l:l + 1].to_broadcast([64, 128]),
           rhs=Wk[64:128, l, :], start=(l == 0), stop=(l == 31))
    if DBG["cmp"]:
        V("tensor_copy", out=biasv[:], in_=bvp[:, 0:64])
    for nt in range(2 if DBG["cmp"] else 0):
        nn = 128 if nt == 0 else 127
        vps = banks[3 + nt]
        for l in range(32):
            s0 = nt * 2048 + l
            PE(out=vps[0:nn, 0:64], lhsT=kvcT[64:128, s0:s0 + 16 * (nn - 1) + 1:16], rhs=Wk[64:128, l, :],
               start=(l == 0), stop=(l == 31))
        V("tensor_tensor", out=Vc[0:nn, nt, 0:64], in0=vps[0:nn, 0:64], in1=biasv[0:nn, :], op=ALU.add)
        V("memset", ap=Vc[0:nn, nt, 64:128], constant=1.0)

    mk.end_phase()
    es.close()

    es = ExitStack()
    PT = [T(f"PT{i}", [128, 512], BF16) for i in range(4)]
    PTm = T("PTm", [128, 512], BF16)
    Osb = [[T(f"Osb{i}_{j}", [128, 512], F32) for j in range(2)] for i in range(3)]
    PTmm = [T(f"PTm{j}", [128, 512], BF16) for j in range(2)]
    c_sbs = [T(f"c_sb{j}", [64, 512], F32) for j in range(3)]
    y_sb = T("y_sb", [128, 256], F32)
    tmp_sbs = [T(f"tmp_sb{j}", [128, 256], F32) for j in range(3)]
    A_sb = T("A_sb", [128, 260], F32)
    rdc = T("rdc", [128, 4], F32)
    imp = T("imp", [128, 64], F32)
    imp2 = T("imp2", [128, 64], F32)
    m8 = T("m8", [128, 16], F32)
    nsel = T("nsel", [128, 64], F32)
    qn = T("qn", [128, 2, 512], BF16)
    POOL("memset", ap=qn[:], constant=0.0)

    ST = [banks[0], banks[1]]
    OsB = [banks[2], banks[2]]
    OcwB = banks[3]
    gbBs = [banks[4], banks[5], banks[6]]
    A_ps = banks[7][:, 0:260]
    nT_ps = banks[7][0:64, 384:512]
    sti = [0]
    pti = [0]

    def make_qbd(i):
        qb = qn[0:64, i % 2, :].rearrange("p (j r q) -> p j r q", j=2, r=2)
        POOL("tensor_copy", out=qb[:, :, 0, :], in_=qT[0:64, i, :, :])
        POOL("tensor_copy", out=qb[:, :, 1, :], in_=qT[64:128, i, :, :])

    def score_tile(kT_sb, k0, i, mask=None, dst=None):
        st = ST[sti[0] % 2]
        sti[0] += 1
        PE(out=st[:], lhsT=kT_sb[:, k0:k0 + 128], rhs=qn[:, i % 2, :], start=True, stop=(mask is None), skip_group_check=True)
        if mask is not None:
            lhsT, rhs = mask
            PE(out=st[:], lhsT=lhsT, rhs=rhs, start=False, stop=True, skip_group_check=True)
        if dst is None:
            pt = PT[pti[0] % 4]
            pti[0] += 1
        else:
            pt = dst
        ACT(out=pt[:], in_=st[:], func=AF.Exp, scale=0.125)
        return pt

    def bc4(ap2d, p):
        return ap2d.unsqueeze(1).to_broadcast([p, 4, 128])

    cmp_ptm = {}

    def cmp_A(i):
        make_qbd(i)
        nts = [0] if i < 16 else [0, 1]
        lst = []
        for nt in nts:
            if nt == 0 and i >= 17:
                ptm = score_tile(kcT, nt * 128, i, dst=PTmm[nt])
            else:
                pt = score_tile(kcT, nt * 128, i)
                c0 = 128 * i if nt == 0 else 128 * (i - 16)
                ptm = PTmm[nt]
                V("tensor_tensor", out=ptm[:].rearrange("p (h q) -> p h q", h=4), in0=pt[:].rearrange("p (h q) -> p h q", h=4),
                  in1=bc4(mbig[:, c0:c0 + 128], 128), op=ALU.mult)
            lst.append(ptm)
        cmp_ptm[i] = lst

    def cmp_B(i):
        nts = [0] if i < 16 else [0, 1]
        for nt in nts:
            ptm = cmp_ptm[i][nt]
            PE(out=OcwB[:], lhsT=Vc[:, nt, :], rhs=ptm[:], start=(nt == 0), stop=(nt == nts[-1]))
            for h in range(4):
                PE(out=A_ps[:, h * 65:(h + 1) * 65], lhsT=ptm[:, h * 128:(h + 1) * 128], rhs=selmap[:, nt * 65:(nt + 1) * 65],
                   start=(nt == 0 and h == 0), stop=(nt == nts[-1]), skip_group_check=True)
        ACT(out=Osb[0][i % 2][:], in_=OcwB[:], func=AF.Identity, bias=tiny[:])
        V("tensor_copy", out=A_sb[:], in_=A_ps)
        A3 = A_sb[:].rearrange("p (h c) -> p h c", h=4)
        V("tensor_scalar", out=rdc[:], in0=A3[:, :, 64], scalar1=1e-30, scalar2=None, op0=ALU.max)
        V("reciprocal", out=rdc[:], in_=rdc[:])
        V("tensor_scalar", out=imp[:], in0=A3[:, 0, 0:64], scalar1=rdc[:, 0:1], scalar2=None, op0=ALU.mult)
        for h in range(1, 4):
            V("scalar_tensor_tensor", out=imp[:], in0=A3[:, h, 0:64], scalar=rdc[:, h:h + 1], in1=imp[:],
              op0=ALU.mult, op1=ALU.add)
        cc = 62 - 2 * i
        V("tensor_tensor", out=imp[:], in0=imp[:], in1=cmb[:, cc:cc + 64], op=ALU.mult)
        V("tensor_tensor", out=imp[:], in0=imp[:], in1=amb[:, cc:cc + 64], op=ALU.add)
        V("memset", ap=imp[:, 0:1], constant=3.0e4)
        V("max", out=m8[:, 0:8], in_=imp[:])
        V("match_replace", out=imp2[:], in_to_replace=m8[:, 0:8], in_values=imp[:], imm_value=-3.0e4)
        V("max", out=m8[:, 8:16], in_=imp2[:])
        V("tensor_scalar", out=nsel[:], in0=imp[:], scalar1=m8[:, 15:16], scalar2=NEG, op0=ALU.is_lt, op1=ALU.mult)

    def cmp_C(i):
        PE("transpose", out=nT_ps, in_=nsel[:], identity=identf[:])
        ACT(out=qn[64:128, i % 2, :].rearrange("p (h q) -> p h q", h=4), in_=nT_ps.unsqueeze(1).to_broadcast([64, 4, 128]),
            func=AF.Copy)

    def attn_loop(kT_sb, Vaug, kts, i, Obank, maskf):
        pts = []
        n = len(kts)
        pend = None
        for idx, kt in enumerate(kts):
            pt = score_tile(kT_sb, kt * 128, i, maskf(kt))
            if pend is not None:
                pidx, pkt, ppt = pend
                PE(out=Obank[:], lhsT=Vaug[:, pkt, :], rhs=ppt[:], start=(pidx == 0), stop=False)
            pend = (idx, kt, pt)
        pidx, pkt, ppt = pend
        PE(out=Obank[:], lhsT=Vaug[:, pkt, :], rhs=ppt[:], start=(pidx == 0), stop=True)

    def sel_mask(i):
        def f(kt):
            if kt == i:
                return (identb[:], negtri[:])
            return None
        return f

    def win_mask(i):
        def f(kt):
            if kt == i:
                return (identb[:], negtri[:])
            if kt == i - 4:
                return (identb[:], negtriw[:])
            return None
        return f

    def combine(i, osb_list):
        pg = 0
        c0 = i * 128
        for n_, (br, osb) in enumerate(osb_list):
            gbB = gbBs[br]
            for cb in range(4):
                h = cb
                hb = h * 3 + br
                lhsT = identb[pg:pg + 12, pg + hb:pg + hb + 1].to_broadcast([12, 64])
                PE(out=gbB[0:64, cb * 128:(cb + 1) * 128], lhsT=lhsT, rhs=gl[pg:pg + 12, c0:c0 + 128], start=True, stop=True)
        for n_, (br, osb) in enumerate(osb_list):
            gbB = gbBs[br]
            c_sb = c_sbs[br]
            tmp_sb = tmp_sbs[br]
            if i < 16 and br != 0:
                ACT(out=c_sb[:], in_=osb[64:128, :], func=AF.Ln)
                ACT(out=c_sb[:], in_=c_sb[:], func=AF.Exp, scale=-1.0)
            else:
                V("reciprocal", out=c_sb[:], in_=osb[64:128, :])
            V("tensor_tensor", out=c_sb[:], in0=gbB[0:64, :], in1=c_sb[:], op=ALU.mult)
            dst = y_sb if n_ == 0 else tmp_sb
            o4 = osb[0:64, :].rearrange("p (j r q) -> p j r q", j=2, r=2)
            c4 = c_sb[:].rearrange("p (j r q) -> p j r q", j=2, r=2)
            for hh in range(2):
                POOL("tensor_tensor", out=dst[64 * hh:64 * hh + 64, :].rearrange("p (j q) -> p j q", j=2), in0=o4[:, :, hh, :],
                     in1=c4[:, :, hh, :], op=ALU.mult)
            if n_ > 0:
                POOL("tensor_tensor", out=y_sb[:], in0=y_sb[:], in1=tmp_sb[:], op=ALU.add)
        V("tensor_tensor", out=YT[:, 1:3, i * 128:(i + 1) * 128], in0=y_sb[:].rearrange("p (j q) -> p j q", j=2),
          in1=ngs[:, :, i * 128:(i + 1) * 128], op=ALU.mult)

    NQ = DBG["nq"]
    if NQ > 0:
        cmp_A(0)
        cmp_B(0)
    for i in range(NQ):
        if i + 1 < NQ:
            cmp_A(i + 1)
        cmp_C(i)
        if i > 0:
            combine(i - 1, [(b_, Osb[b_][(i - 1) % 2]) for b_ in DBG.get("branches", (0, 1, 2))])
        kts = [kt for kt in range(i - 4, i + 1) if kt >= 0]
        attn_loop(kwT, Vw, kts, i, OcwB, win_mask(i))
        ACT(out=Osb[2][i % 2][:], in_=OcwB[:], func=AF.Identity, bias=tiny[:])
        if i + 1 < NQ:
            cmp_B(i + 1)
        attn_loop(ksT, Vs, list(range(i + 1)), i, OsB[i % 2], sel_mask(i))
        ACT(out=Osb[1][i % 2][:], in_=OsB[i % 2][:], func=AF.Identity, bias=tiny[:])
    if NQ > 0:
        combine(NQ - 1, [(b_, Osb[b_][(NQ - 1) % 2]) for b_ in DBG.get("branches", (0, 1, 2))])

    for c in range(4):
        mk.dma("sync", out=yt_d[c * 128:(c + 1) * 128, :], in_=YT[:, c, :], _force=True)
    mk.end_phase()
    es.close()
    pes.close()


def emit_B(nc, mk, banks, sfx, D, make_xT):
    yt_d = D["ytf"]; x_d = D["xr"]; wo_d = D["wo"]; g_d = D["lng"]; b_d = D["lnb"]; o_d = D["xo"]
    mk.sfx = sfx
    es = ExitStack()

    def T(name, shape, dt):
        return es.enter_context(nc.sbuf_tensor(name + sfx, list(shape), dt))

    wo = T("wo_sb", [128, 8, DM], BF16)
    gam = T("gam", [128, DM], F32)
    bet = T("bet", [128, DM], F32)
    ytb = T("ytb", [128, 2, 8, 512], BF16)
    xs = [T(f"xs{i}", [128, DM], F32) for i in range(3)]
    zs = [T(f"zs{i}", [128, DM], F32) for i in range(3)]
    ys = [T(f"ys{i}", [128, DM], F32) for i in range(2)]
    identb = T("identb_b", [128, 128], BF16)
    zb = T("zb", [128, DM], BF16)
    xTt = T("xTt", [128, 2, 8, 128], BF16)
    alpha = float((2.0 * 2) ** 0.25)

    mk.dma("gpsimd", out=wo[:, 0:4, :], in_=wo_d.rearrange("(c p) n -> p c n", p=128)[:, 0:4, :])
    mk.dma("gpsimd", out=wo[:, 4:8, :], in_=wo_d.rearrange("(c p) n -> p c n", p=128)[:, 4:8, :])
    mk.dma("gpsimd", out=identb[:], in_=D["ident"])
    mk.dma("sync", out=gam[:], in_=g_d.to_broadcast([128, DM]))
    mk.dma("sync", out=bet[:], in_=b_d.to_broadcast([128, DM]))
    yt_v = yt_d.rearrange("(c p) t -> p c t", p=128)
    xT_v = D["xTo"].rearrange("(c p) t -> p c t", p=128) if make_xT else None
    st6s = [T(f"bst6_{i}", [128, 2, 6], F32) for i in range(2)]
    mvs = [T(f"bmv_{i}", [128, 2], F32) for i in range(2)]
    rstds = [T(f"brstd_{i}", [128, 1], F32) for i in range(2)]
    NTB = S // 128

    def stage1(t):
        tb = t // 4
        if t % 4 == 0:
            for tbn in ([0, 1] if tb == 0 else [tb + 1]):
                if tbn < S // 512:
                    mk.dma("scalar", out=ytb[:, tbn % 2, 0:4, :], in_=yt_v[:, 0:4, tbn * 512:(tbn + 1) * 512], key="ytb%d" % (tbn % 2))
                    mk.dma("scalar", out=ytb[:, tbn % 2, 4:8, :], in_=yt_v[:, 4:8, tbn * 512:(tbn + 1) * 512], key="ytb%d" % (tbn % 2))
        x_sb = xs[t % 3]
        z = zs[t % 3]
        y_sb = ys[t % 2]
        mk.dma("scalar", out=x_sb[:], in_=x_d[t * 128:(t + 1) * 128, :])
        nbk = 6 if make_xT else 8
        for nh in range(2):
            ps = banks[(2 * t + nh) % nbk]
            for c in range(8):
                mk.op("tensor", "matmul", out=ps[:], lhsT=ytb[:, tb % 2, c, (t % 4) * 128:(t % 4 + 1) * 128],
                      rhs=wo[:, c, nh * 512:(nh + 1) * 512], start=(c == 0), stop=(c == 7))
            mk.op("scalar", "activation", out=y_sb[:, nh * 512:(nh + 1) * 512], in_=ps[:], func=AF.Copy)
            mk.op("vector", "scalar_tensor_tensor", out=z[:, nh * 512:(nh + 1) * 512], in0=x_sb[:, nh * 512:(nh + 1) * 512],
                  scalar=alpha, in1=y_sb[:, nh * 512:(nh + 1) * 512], op0=ALU.mult, op1=ALU.add)
            mk.op("vector", "bn_stats", out=st6s[t % 2][:, nh, :], in_=z[:, nh * 512:(nh + 1) * 512])
        mk.op("vector", "bn_aggr", out=mvs[t % 2][:], in_=st6s[t % 2][:].rearrange("p a b -> p (a b)"))
        mk.op("vector", "tensor_scalar", out=rstds[t % 2][:], in0=mvs[t % 2][:, 1:2], scalar1=1e-5, scalar2=None, op0=ALU.add)

    def stage2(t):
        z = zs[t % 3]
        mv = mvs[t % 2]
        rstd = rstds[t % 2]
        mk.op("scalar", "activation", out=rstd[:], in_=rstd[:], func=AF.Sqrt)
        mk.op("vector", "reciprocal", out=rstd[:], in_=rstd[:])
        mk.op("vector", "tensor_scalar", out=z[:], in0=z[:], scalar1=mv[:, 0:1], scalar2=rstd[:, 0:1], op0=ALU.subtract, op1=ALU.mult)
        mk.op("gpsimd", "tensor_tensor", out=z[:], in0=z[:], in1=gam[:], op=ALU.mult)
        mk.op("gpsimd", "tensor_tensor", out=z[:], in0=z[:], in1=bet[:], op=ALU.add)
        mk.dma("sync", out=o_d[t * 128:(t + 1) * 128, :], in_=z[:], key="outz%d" % (t % 3))
        if make_xT:
            mk.op("scalar", "activation", out=zb[:], in_=z[:], func=AF.Copy)
            bT = banks[6 + t % 2][:, :].bitcast(BF16)
            for c in range(8):
                mk.op("tensor", "transpose", out=bT[:, c * 128:(c + 1) * 128], in_=zb[:, c * 128:(c + 1) * 128], identity=identb[:])
            mk.op("scalar", "activation", out=xTt[:, t % 2].rearrange("p c q -> p (c q)"), in_=bT, func=AF.Copy)
            mk.dma("sync", out=xT_v[:, :, t * 128:(t + 1) * 128], in_=xTt[:, t % 2], key="xTo%d" % (t % 2))

    stage1(0)
    for t in range(NTB):
        if t + 1 < NTB:
            stage1(t + 1)
        stage2(t)
    mk.end_phase()
    es.close()


def build_F():
    nc = bass.Bass("TRN2", target_bir_lowering=False)

    def din(name, shape, dt=F32):
        return nc.dram_tensor(name, list(shape), dt, kind="ExternalInput").ap()

    xT0 = din("xT0", [DM, S]); x0 = din("x0", [S, DM])
    w_d = din("w", [4, DM, NCOL])
    cos_d = din("cos", [S, 32]); sin_d = din("sin", [S, 32])
    kd_d = din("kd", [2, 128, 2]); qd_d = din("qd", [2, 128, 128]); dt_d = din("dt", [2, 128, 256]); cdv_d = din("cdv", [2, 128, 1])
    mbig_d = din("mbig", [128, 2176]); negtri_d = din("negtri", [128, 512]); negtriw_d = din("negtriw", [128, 512])
    ident_d = din("ident", [128, 128]); selmap_d = din("selmap", [128, 130])
    cmb_d = din("cmb", [128, 126]); amb_d = din("amb", [128, 126]); esel_d = din("esel", [64, 4096])
    cmpw_d = din("cmpw", [2, 2, 2048, 64]); posT_d = din("posT", [2, 2, 64, 32])
    lruv_d = din("lruv", [4, 128, 9]); wa_d = din("wa", [4, 128, 128]); wx_d = din("wx", [4, 128, 128])
    wo_d = din("wo", [2, DM, DM]); g_d = din("lng", [2, 1, DM]); b_d = din("lnb", [2, 1, DM])
    xo = nc.dram_tensor("xo", [S, DM], F32, kind="ExternalOutput").ap()
    ytf = nc.dram_tensor("ytf_scr", [DM, S], BF16).ap()
    x1 = nc.dram_tensor("x1_scr", [S, DM], F32).ap()
    x1T = nc.dram_tensor("x1T_scr", [DM, S], BF16).ap()
    mk = MK(nc)
    banks = [nc.alloc_psum_tensor(f"bank{i}", [128, 512], F32) for i in range(8)]
    for l in range(DBG.get("nlayers", 2)):
        for hf in range(2):
            i4 = l * 2 + hf
            D = dict(xT=(xT0 if l == 0 else x1T), w=w_d[i4], cos=cos_d, sin=sin_d, kd=kd_d[hf], qd=qd_d[hf], dt=dt_d[hf],
                     cdv=cdv_d[hf], mbig=mbig_d, negtri=negtri_d, negtriw=negtriw_d, ident=ident_d, selmap=selmap_d,
                     cmb=cmb_d, amb=amb_d, esel=esel_d, cmpw=cmpw_d[l], posT=posT_d[l], lruv=lruv_d[i4], wa=wa_d[i4],
                     wx=wx_d[i4], yt=ytf[hf * 512:(hf + 1) * 512, :])
            emit_A(nc, mk, banks, "_a%d" % i4, D)
        last = (l == DBG.get("nlayers", 2) - 1)
        D = dict(ytf=ytf, xr=(x0 if l == 0 else x1), wo=wo_d[l], lng=g_d[l], lnb=b_d[l], xo=(xo if last else x1),
                 ident=ident_d, xTo=x1T)
        emit_B(nc, mk, banks, "_b%d" % l, D, make_xT=not last)
    return nc, mk


def f_inputs(inp, b):
    c0, c1 = make_consts(0), make_consts(1)
    m = {k: c0[k] for k in ("cos", "sin", "mbig", "negtri", "negtriw", "ident", "selmap", "cmb", "amb", "esel")}
    for k in ("kd", "qd", "dt", "cdv"):
        m[k] = np.ascontiguousarray(np.stack([c0[k], c1[k]]))
    xb = np.asarray(inp["x"][b], np.float32)
    m["x0"] = np.ascontiguousarray(xb)
    m["xT0"] = np.ascontiguousarray(xb.T)
    m["w"] = np.ascontiguousarray(np.stack([inp["w_in"][l][:, core_columns(hf)] for l in range(2) for hf in range(2)]))
    m["cmpw"] = np.ascontiguousarray(inp["nsa_cmp_w"])
    m["posT"] = np.ascontiguousarray(inp["nsa_cmp_pos"].transpose(0, 1, 3, 2))
    lruv = np.zeros((4, 128, 9), np.float32)
    wa = np.zeros((4, 128, 128), np.float32)
    wx = np.zeros((4, 128, 128), np.float32)
    for l in range(2):
        for hf in range(2):
            i4 = l * 2 + hf
            ch = slice(hf * 128, hf * 128 + 128)
            lruv[i4, :, 0:4] = inp["lru_conv_w"][l][:, ch].T
            lruv[i4, :, 4] = inp["lru_conv_b"][l][ch]
            lruv[i4, :, 5] = inp["lru_b_a"][l][ch]
            lruv[i4, :, 6] = inp["lru_b_x"][l][ch]
            lruv[i4, :, 7] = inp["lru_lambda"][l][ch]
            for j in range(2):
                wa[i4, j * 64:(j + 1) * 64, j * 64:(j + 1) * 64] = inp["lru_w_a"][l][2 * hf + j]
                wx[i4, j * 64:(j + 1) * 64, j * 64:(j + 1) * 64] = inp["lru_w_x"][l][2 * hf + j]
    m["lruv"] = lruv; m["wa"] = wa; m["wx"] = wx
    rows = np.concatenate([wout_rows(0), wout_rows(1)])
    m["wo"] = np.ascontiguousarray(np.stack([inp["w_out"][l][rows, :] for l in range(2)]))
    m["lng"] = np.ascontiguousarray(inp["ln_g"][:, None, :])
    m["lnb"] = np.ascontiguousarray(inp["ln_b"][:, None, :])
    return m


_PROG = {}


def kernel(**inputs):
    inp = {k: np.asarray(v) for k, v in inputs.items()}
    if "F" not in _PROG:
        _PROG["F"] = build_F()[0]
    in_maps = [f_inputs(inp, b) for b in range(4)]
    res = run_bass_kernel_spmd(_PROG["F"], in_maps, core_ids=list(range(4)))
    return np.stack([np.asarray(res.results[b]["xo"], dtype=np.float32) for b in range(4)])
```
